# Optimizing a Trainium2 kernel written in Bass

```python
import math
import jax
import jax.numpy as jnp
from jax import lax
import numpy as np

D_MODEL = 4096
BATCH = 1
SEQ = 8192
DEPTH = 1

DA_HEADS = 16
DA_HEAD_DIM = 128
DA_V_DIM = 2 * DA_HEAD_DIM
DA_WIDTH = DA_HEADS * DA_V_DIM
SG_WIDTH = 4096
SG_GROUPS = 16
SG_GROUP_DIM = SG_WIDTH // SG_GROUPS
CHUNK = 128
NUM_BUCKETS = 32
MAX_DISTANCE = 128
MEM_LEN = 256
XA_HEADS = 4
XA_HEAD_DIM = 128
XA_WIDTH = XA_HEADS * XA_HEAD_DIM
Q_BLOCK = 128
EPS = 1e-6
IN_SIZES = (DA_WIDTH, DA_WIDTH, DA_WIDTH, DA_WIDTH, SG_WIDTH, SG_WIDTH, SG_WIDTH, 2 * D_MODEL)
IN_WIDTH = 4 * DA_WIDTH + 3 * SG_WIDTH + 2 * D_MODEL

kernel_name = "hybrid_diffattn_chunk_sgu_block"


def _split_points(sizes):
    pts, acc = [], 0
    for s in sizes[:-1]:
        acc += s
        pts.append(acc)
    return pts


def rms_norm(x, g):
    xf = x.astype(jnp.float32)
    y = xf * lax.rsqrt(jnp.mean(xf * xf, axis=-1, keepdims=True) + EPS)
    return (y * g.astype(jnp.float32)).astype(x.dtype)


def layer_norm(x, g, b):
    xf = x.astype(jnp.float32)
    mu = jnp.mean(xf, axis=-1, keepdims=True)
    xc = xf - mu
    y = xc * lax.rsqrt(jnp.mean(xc * xc, axis=-1, keepdims=True) + EPS)
    return (y * g.astype(jnp.float32) + b.astype(jnp.float32)).astype(x.dtype)


def t5_bucket(rel):
    n = jnp.maximum(rel, 0)
    max_exact = NUM_BUCKETS // 2
    nf = jnp.maximum(n, 1).astype(jnp.float32)
    large = max_exact + (jnp.log(nf / max_exact) / math.log(MAX_DISTANCE / max_exact)
                         * (NUM_BUCKETS - max_exact)).astype(jnp.int32)
    large = jnp.minimum(large, NUM_BUCKETS - 1)
    return jnp.where(n < max_exact, n, large)


def diff_attention(q, k, v, positions, rel_bias, lam):
    B, S, H, _, Dh = q.shape
    nb = S // Q_BLOCK
    scale = Dh ** -0.5
    qb = q.reshape(B, nb, Q_BLOCK, H, 2, Dh).transpose(1, 0, 2, 3, 4, 5)
    pb = positions.reshape(B, nb, Q_BLOCK).transpose(1, 0, 2)

    def block(args):
        qi, pi = args
        s = jnp.einsum('bqhcd,bkhcd->bchqk', qi, k).astype(jnp.float32) * scale
        rel = pi[:, :, None] - positions[:, None, :]
        bias = rel_bias[t5_bucket(rel)]
        bias = bias.transpose(0, 3, 1, 2)[:, None].astype(jnp.float32)
        s = jnp.where((rel >= 0)[:, None, None], s + bias, -jnp.inf)
        p = jax.nn.softmax(s, axis=-1)
        a = p[:, 0] - lam * p[:, 1]
        return jnp.einsum('bhqk,bkhd->bqhd', a.astype(v.dtype), v)

    o = lax.map(block, (qb, pb))
    return o.transpose(1, 0, 2, 3, 4).reshape(B, S, H, v.shape[-1])


def chunk_spatial_gate(u, v, w_s, b_s, ln_g, ln_b):
    B, S, _ = v.shape
    v = layer_norm(v, ln_g, ln_b)
    vc = v.reshape(B, S // CHUNK, CHUNK, SG_GROUPS, SG_GROUP_DIM)
    mask = jnp.tril(jnp.ones((CHUNK, CHUNK), dtype=bool))
    w = jnp.where(mask[None], w_s, jnp.zeros_like(w_s))
    mixed = jnp.einsum('gts,bnsgd->bntgd', w, vc) + b_s.T[None, None, :, :, None]
    return u * mixed.reshape(B, S, SG_WIDTH)


def memory_cross_attention(h, mem, w_xq, w_xkv, w_xo):
    B, S, _ = h.shape
    q = (h @ w_xq).reshape(B, S, XA_HEADS, XA_HEAD_DIM)
    kv = (mem @ w_xkv).reshape(B, MEM_LEN, 2, XA_HEADS, XA_HEAD_DIM)
    k, v = kv[:, :, 0], kv[:, :, 1]
    s = jnp.einsum('bqhd,bkhd->bhqk', q, k).astype(jnp.float32) * (XA_HEAD_DIM ** -0.5)
    p = jax.nn.softmax(s, axis=-1).astype(v.dtype)
    o = jnp.einsum('bhqk,bkhd->bqhd', p, v).reshape(B, S, XA_WIDTH)
    return o @ w_xo


def setup_inputs(seed: int = 0) -> dict:
    key = jax.random.key(seed)
    ks = jax.random.split(key, 26)
    f = jnp.float32

    def nrm(k, shape, scale):
        return jax.random.normal(k, shape, f) * scale

    x = nrm(ks[0], (BATCH, SEQ, D_MODEL), 1.0)
    start = jax.random.randint(ks[1], (BATCH, 1), 0, 4096, dtype=jnp.int32)
    positions = start + jnp.arange(SEQ, dtype=jnp.int32)[None, :]
    mem = nrm(ks[2], (BATCH, MEM_LEN, D_MODEL), 1.0)
    return {
        "x": x,
        "positions": positions,
        "mem": mem,
        "w_in": nrm(ks[3], (DEPTH, D_MODEL, IN_WIDTH), D_MODEL ** -0.5),
        "b_gate": nrm(ks[4], (DEPTH, 2 * D_MODEL), 0.1),
        "norm1_g": 1.0 + nrm(ks[5], (DEPTH, D_MODEL), 0.02),
        "lam_q1": nrm(ks[6], (DEPTH, DA_HEAD_DIM), 0.1),
        "lam_k1": nrm(ks[7], (DEPTH, DA_HEAD_DIM), 0.1),
        "lam_q2": nrm(ks[8], (DEPTH, DA_HEAD_DIM), 0.1),
        "lam_k2": nrm(ks[9], (DEPTH, DA_HEAD_DIM), 0.1),
        "subln_g": 1.0 + nrm(ks[10], (DEPTH, DA_V_DIM), 0.02),
        "rel_bias": nrm(ks[11], (NUM_BUCKETS, DA_HEADS), 0.5),
        "sg_ln_g": 1.0 + nrm(ks[12], (DEPTH, SG_WIDTH), 0.02),
        "sg_ln_b": nrm(ks[13], (DEPTH, SG_WIDTH), 0.02),
        "sg_w": nrm(ks[14], (DEPTH, SG_GROUPS, CHUNK, CHUNK), CHUNK ** -0.5),
        "sg_b": 1.0 + nrm(ks[15], (DEPTH, SG_GROUPS, CHUNK), 0.1),
        "w_proj_a": nrm(ks[16], (DEPTH, DA_WIDTH, D_MODEL), DA_WIDTH ** -0.5),
        "w_proj_b": nrm(ks[17], (DEPTH, SG_WIDTH, D_MODEL), SG_WIDTH ** -0.5),
        "w_out": nrm(ks[18], (DEPTH, D_MODEL, D_MODEL), D_MODEL ** -0.5),
        "norm_x_g": 1.0 + nrm(ks[19], (DEPTH, D_MODEL), 0.02),
        "norm_mem_g": 1.0 + nrm(ks[20], (DEPTH, D_MODEL), 0.02),
        "w_xq": nrm(ks[21], (DEPTH, D_MODEL, XA_WIDTH), D_MODEL ** -0.5),
        "w_xkv": nrm(ks[22], (DEPTH, D_MODEL, 2 * XA_WIDTH), D_MODEL ** -0.5),
        "w_xo": nrm(ks[23], (DEPTH, XA_WIDTH, D_MODEL), XA_WIDTH ** -0.5),
        "final_g": 1.0 + nrm(ks[24], (D_MODEL,), 0.02),
    }


def reference(x, positions, mem, w_in, b_gate, norm1_g, lam_q1, lam_k1, lam_q2, lam_k2,
              subln_g, rel_bias, sg_ln_g, sg_ln_b, sg_w, sg_b, w_proj_a, w_proj_b, w_out,
              norm_x_g, norm_mem_g, w_xq, w_xkv, w_xo, final_g):
    B, S, _ = x.shape
    f32 = jnp.float32
    h = x
    for l in range(DEPTH):
        lam_init = 0.8 - 0.6 * math.exp(-0.3 * l)
        hn = rms_norm(h, norm1_g[l])
        proj = hn @ w_in[l]
        q_a, k_a, v_a, z_a, u_b, v_b, z_b, g_pre = jnp.split(proj, _split_points(IN_SIZES), axis=-1)
        lam = (jnp.exp(jnp.sum(lam_q1[l].astype(f32) * lam_k1[l].astype(f32)))
               - jnp.exp(jnp.sum(lam_q2[l].astype(f32) * lam_k2[l].astype(f32))) + lam_init)
        q_a = q_a.reshape(B, S, DA_HEADS, 2, DA_HEAD_DIM)
        k_a = k_a.reshape(B, S, DA_HEADS, 2, DA_HEAD_DIM)
        v_a = v_a.reshape(B, S, DA_HEADS, DA_V_DIM)
        o_a = diff_attention(q_a, k_a, v_a, positions, rel_bias, lam)
        o_a = (rms_norm(o_a, subln_g[l]) * (1.0 - lam_init)).reshape(B, S, DA_WIDTH)
        y_a = (o_a * jax.nn.silu(z_a)) @ w_proj_a[l]
        o_b = chunk_spatial_gate(jax.nn.gelu(u_b), jax.nn.gelu(v_b), sg_w[l], sg_b[l],
                                 sg_ln_g[l], sg_ln_b[l])
        y_b = (o_b * jax.nn.silu(z_b)) @ w_proj_b[l]
        g = jax.nn.sigmoid(g_pre + b_gate[l])
        g_a, g_b = g[..., :D_MODEL], g[..., D_MODEL:]
        h = h + (g_a * y_a + g_b * y_b) @ w_out[l]
        hn = rms_norm(h, norm_x_g[l])
        mn = rms_norm(mem, norm_mem_g[l])
        h = h + memory_cross_attention(hn, mn, w_xq[l], w_xkv[l], w_xo[l])
    return rms_norm(h, final_g)
```

```python
import math
import types
from contextlib import ExitStack
import numpy as np
import ml_dtypes
import concourse.bass as bass
import concourse.mybir as mybir
from concourse.bass_utils import run_bass_kernel_spmd

F32 = mybir.dt.float32
BF16 = mybir.dt.bfloat16
AF = mybir.ActivationFunctionType
ALU = mybir.AluOpType

ENGS = ["pe", "act", "dve", "pool", "sp"]
NEG = -30000.0
EPS = 1e-6


class G:
    def __init__(self, nc, stack, n_dma_sems=48):
        self.nc = nc
        self.esem = {e: stack.enter_context(nc.semaphore("s_" + e)) for e in ENGS}
        self.ecount = {e: 0 for e in ENGS}
        self.dsem_pool = [stack.enter_context(nc.semaphore("d%d" % i)) for i in range(n_dma_sems)]
        self.dsem = {}
        self.dcount = {}
        self.frontier = {}

    def get_dsem(self, name):
        if name not in self.dsem:
            self.dsem[name] = self.dsem_pool[len(self.dsem)]
            self.dcount[name] = 0
        return self.dsem[name]


def _snap(fn):
    if fn.__closure__ is None:
        return fn
    cells = []
    for c in fn.__closure__:
        try:
            cells.append(types.CellType(c.cell_contents))
        except ValueError:
            cells.append(c)
    g = types.FunctionType(fn.__code__, fn.__globals__, fn.__name__, fn.__defaults__, tuple(cells))
    g.__kwdefaults__ = fn.__kwdefaults__
    return g


def _norm_tokens(toks):
    out = []
    for t in toks:
        if (isinstance(t, tuple) and t[0] == "AT") or t == "KTs":
            out.append("AT")
        elif t == "Vt":
            out += [("WB", 0), ("WB", 1)]
        elif t in ("wsT", "CB"):
            out.append(("WB", 0))
        elif t == "qxT":
            out += [("ET", i) for i in range(4)]
        elif t == "KxT":
            out += [("EU", i) for i in range(3)]
        elif t == "Vx":
            out += [("OS", i) for i in range(4)]
        elif t in ("BR", "MK", "lam4"):
            out.append(("FA", 1))
        else:
            out.append(t)
    return tuple(dict.fromkeys(out))


class Op:
    __slots__ = ("kind", "eng", "fn", "r", "w", "sem", "deps", "signal", "sigval", "inc")

    def __init__(self, kind, eng, fn, r, w, sem=None, inc=16):
        self.kind, self.eng, self.fn, self.r, self.w, self.sem = kind, eng, _snap(fn), _norm_tokens(r), _norm_tokens(w), sem
        self.deps = []
        self.signal = kind != "c"
        self.sigval = None
        self.inc = inc


class Phase:
    def __init__(self, g):
        self.g = g
        self.ops = []

    def op(self, eng, fn, r=(), w=()):
        self.ops.append(Op("c", eng, fn, r, w))

    def dma(self, q, fn, r=(), w=(), sem=None, inc=16):
        self.g.get_dsem(sem)
        self.ops.append(Op("d", q, fn, r, w, sem, inc))

    def emit(self):
        g = self.g
        nc = g.nc
        lastw, readers = {}, {}
        ops = self.ops
        for i, o in enumerate(ops):
            deps = set()
            for t in o.r:
                if t in lastw:
                    deps.add(lastw[t])
            for t in o.w:
                if t in lastw:
                    deps.add(lastw[t])
                for rd in readers.get(t, ()):
                    deps.add(rd)
            deps.discard(i)
            for d in sorted(deps):
                p = ops[d]
                if p.kind == "c" and o.kind == "c" and p.eng == o.eng == "pe":
                    continue
                if p.kind == "d" and o.kind == "d" and p.sem == o.sem == "su":
                    continue
                o.deps.append(d)
                p.signal = True
            for t in o.r:
                readers.setdefault(t, []).append(i)
            for t in o.w:
                lastw[t] = i
                readers[t] = []
        for o in ops:
            if o.kind == "c":
                if o.signal:
                    g.ecount[o.eng] += 1
                    o.sigval = g.ecount[o.eng]
            else:
                g.dcount[o.sem] += o.inc
                o.sigval = g.dcount[o.sem]
                assert o.sigval < 65000, "dma sem overflow %s" % o.sem
        for o in ops:
            if o.kind == "d" and o.sem in ("su",):
                o.sigval = g.dcount[o.sem]
        batches = {}
        qpos = {}
        qseq = {e: 0 for e in ENGS}
        for i, o in enumerate(ops):
            qseq[o.eng] += 1
            qpos[i] = qseq[o.eng]
            if o.kind != "d" or o.sem == "su" or o.inc != 16:
                continue
            cur = batches.setdefault(o.sem, [])
            if cur and (any(d in cur for d in o.deps) or ops[cur[-1]].eng != o.eng or qpos[cur[-1]] + 1 != qpos[i]):
                tot = ops[cur[-1]].sigval
                for j in cur:
                    ops[j].sigval = tot
                cur.clear()
            cur.append(i)
        for sem, cur in batches.items():
            if cur:
                tot = ops[cur[-1]].sigval
                for j in cur:
                    ops[j].sigval = tot
        for e in ENGS:
            assert g.ecount[e] < 65000, "eng sem overflow " + e
        per = {e: [o for o in ops if o.eng == e] for e in ENGS}
        handles = {"pe": "tensor", "act": "scalar", "dve": "vector", "pool": "gpsimd", "sp": "sync"}
        with nc.Block() as block:
            for e in ENGS:
                if not per[e]:
                    continue

                def body(eng, e=e):
                    for o in per[e]:
                        need = {}
                        for d in o.deps:
                            p = ops[d]
                            if p.kind == "c":
                                sem, key = g.esem[p.eng], ("e", p.eng)
                            else:
                                sem, key = g.dsem[p.sem], ("d", p.sem)
                            if key not in need or need[key][1] < p.sigval:
                                need[key] = (sem, p.sigval)
                        for key, (sem, val) in need.items():
                            fk = (e, key)
                            if g.frontier.get(fk, 0) < val:
                                eng.wait_ge(sem, val)
                                g.frontier[fk] = val
                        inst = o.fn(eng)
                        if o.kind == "c":
                            if o.signal:
                                inst.then_inc(g.esem[e], 1)
                        else:
                            inst.then_inc(g.dsem[o.sem], o.inc)

                getattr(block, handles[e])(body)
        self.ops = []


def t5_bucket_np(rel):
    n = np.maximum(rel, 0)
    nf = np.maximum(n, 1).astype(np.float32)
    large = 16 + (np.log(nf / np.float32(16)) / np.float32(math.log(128 / 16)) * np.float32(16)).astype(np.int32)
    large = np.minimum(large, 31)
    return np.where(n < 16, n, large)


def build(D, S):
    KC = D // 128
    TOK = S // 8
    NTB = TOK // 128
    TH = min(512, TOK)
    NTH = TOK // TH
    NB = S // 128
    NQT = S // 256
    GW = 12288 + 2 * D
    nc = bass.Bass("TRN2", target_bir_lowering=False)

    def din(name, shape, dt=F32):
        return nc.dram_tensor(name, list(shape), dt, kind="ExternalInput").ap()

    def dsc(name, shape, dt):
        return nc.dram_tensor(name, list(shape), dt)

    x = din("x", [TOK, D])
    mem = din("mem", [256, D])
    wA = din("wA", [D, 2048])
    wtok = din("wtok", [D // 8, GW])
    wpa = din("wpa", [512, D])
    wpb = din("wpb", [512, D])
    wout = din("wout", [D // 8, D])
    wxq = din("wxq", [D // 8, 512])
    wxkv = din("wxkv", [D // 8, 1024])
    wxo = din("wxo", [64, D])
    norm1_g = din("norm1_g", [D])
    norm_x_g = din("norm_x_g", [D])
    norm_mem_g = din("norm_mem_g", [D])
    final_g = din("final_g", [D])
    b_gate = din("b_gate", [2 * D])
    subln_g = din("subln_g", [256])
    sg_ln_g = din("sg_ln_g", [4096])
    sg_ln_b = din("sg_ln_b", [4096])
    sg_w = din("sg_w", [16, 128, 128])
    sg_b = din("sg_b", [16 * 128])
    lamv = din("lamv", [4, 128])
    braw = din("braw", [2, 128, 512])
    b31 = din("b31", [2, 128, 1])
    cmask = din("cmask", [128, 512])
    tril = din("tril", [128, 128])
    ident_d = din("ident", [128, 128], BF16)
    sel_d = din("sel", [128, 8])
    out = nc.dram_tensor("out", [TOK, D], F32, kind="ExternalOutput").ap()

    hnT_own = dsc("hnT_own", [D, TOK], BF16)
    hnT_all = dsc("hnT_all", [8 * D, TOK], BF16)
    wA_b = dsc("wA_b", [D, 2048], BF16)
    QT = dsc("QT", [4, 128, S], BF16)
    KT = dsc("KT", [4, 128, S], BF16)
    Vd = dsc("Vd", [2, S, 256], BF16)
    ZA = dsc("ZA", [S, 512], F32)
    gaT_own = dsc("gaT_own", [512, S], BF16)
    gaT_all = dsc("gaT_all", [4096, S], BF16)
    usT = dsc("usT", [4096, TOK], F32)
    Vg = dsc("Vg", [TOK, 4096], F32)
    gT = dsc("gT", [2 * D, TOK], F32)
    gbT = dsc("gbT", [4096, TOK], BF16)
    mT = dsc("mT", [D, TOK], F32)
    mTb = dsc("mTb", [D, TOK], BF16)
    h1 = dsc("h1", [TOK, D], F32)
    hn2T = dsc("hn2T", [D, TOK], BF16)
    mnT = dsc("mnT", [D, 256], BF16)
    wspec = [
        ("Wu", wtok[:, 0:4096], D // 8, 4096),
        ("Wz", wtok[:, 8192:12288], D // 8, 4096),
        ("Wv", wtok[:, 4096:8192], D // 8, 4096),
        ("Wga", wtok[:, 12288:12288 + D], D // 8, D),
        ("Wgb", wtok[:, 12288 + D:12288 + 2 * D], D // 8, D),
        ("Wpb", wpb, 512, D),
        ("Wpa", wpa, 512, D),
        ("Wout", wout, D // 8, D),
        ("Wxq", wxq, D // 8, 512),
        ("Wxkv", wxkv, D // 8, 1024),
        ("Wxo", wxo, 64, D),
    ]
    Wsl = {n: dsc(n + "_s", [r, c], BF16) for n, _, r, c in wspec}
    Wf = {n: dsc(n + "_f", [8 * r, c], BF16) for n, _, r, c in wspec}

    st = ExitStack()
    with st:
        g = G(nc, st)
        ph = Phase(g)

        def sb(name, shape, dt):
            return st.enter_context(nc.sbuf_tensor("sb_" + name, list(shape), dt))

        AT = sb("AT", [128, 32 * 1024], BF16)
        WB = sb("WB", [128, 2 * 32 * 512], BF16)
        FA = sb("FA", [128, 2, 4096], F32)
        BA = sb("BA", [128, 2, 4096], BF16)
        ident = sb("ident", [128, 128], BF16)
        ones_b = sb("ones_b", [128, 128], BF16)
        gvec = sb("gvec", [128, 3 * KC], F32)
        bgT = sb("bgT", [128, 2 * KC], F32)
        lgT = sb("lgT", [128, 32], F32)
        lbT = sb("lbT", [128, 32], F32)
        SGt = sb("SGt", [128, 256], F32)
        sm = sb("sm", [128, 64], F32)
        Bh = sb("Bh", [128, 2, 512], F32)
        b31s = sb("b31s", [128, 2], F32)
        selt = sb("selt", [128, 8], F32)
        ET = sb("ET", [128, 4, 512], F32)
        EU = sb("EU", [128, 3, 512], F32)
        PT = sb("PT", [128, 3, 512], BF16)
        QTt = sb("QTt", [128, 2, 2, 256], BF16)
        OS = sb("OS", [128, 4, 260], F32)
        GAs = sb("GAs", [128, 1, 2, 256], BF16)
        PS = [st.enter_context(nc.psum_tensor("ps%d" % i, [128, 512], F32)) for i in range(8)]
        PSB = [st.enter_context(nc.psum_tensor("psb%d" % i, [128, 1024], BF16)) for i in range(0)]

        ETb = ET[:].rearrange("p a n -> p (a n)").bitcast(BF16)
        EUb = EU[:].rearrange("p a n -> p (a n)").bitcast(BF16)
        qxT = ETb[:, 0:4 * TOK].rearrange("p (h t) -> p h t", h=4)
        KxT = EUb[:, 0:1024].rearrange("p (h t) -> p h t", h=4)
        OSb = OS[:].rearrange("p a n -> p (a n)").bitcast(BF16)
        Vx = OSb[:, 0:1056].rearrange("p (b h d) -> p b h d", b=2, h=4)
        wsT = WB[:, 12288:14336].rearrange("p (g t) -> p g t", g=16)
        JK = WB[:, 16384:16384 + 4096]
        ETall = [("ET", i) for i in range(4)]
        EUall = [("EU", i) for i in range(3)]
        OSall = [("OS", i) for i in range(4)]
        lam4 = FA[:, 1, 1664:2176].rearrange("p (a n) -> p a n", a=4)
        BR = FA[:, 1, 128:1152].rearrange("p (a n) -> p a n", a=2)
        MK = FA[:, 1, 1152:1664]
        CB = WB[:, 0:8192].bitcast(F32).rearrange("p (c t) -> p c t", c=32)
        SBv = WB[:, 8192:12288].bitcast(F32)

        cnt = {"ps": 0, "u": 0}

        def uid():
            cnt["u"] += 1
            return cnt["u"]

        def ld(dst, src, w, sem="su", q="sp", r=()):
            ph.dma(q, lambda e: e.dma_start(out=dst, in_=src), r=list(r), w=w, sem=sem)

        with nc.allow_non_contiguous_dma(reason="small strided constant loads"):
            ld(ident[:], ident_d, ["ident"])
            ld(gvec[:, 0:KC], norm1_g.rearrange("(k p) -> p k", p=128), ["gvec"])
            ld(gvec[:, KC:2 * KC], norm_x_g.rearrange("(k p) -> p k", p=128), ["gvec"])
            ld(gvec[:, 2 * KC:3 * KC], norm_mem_g.rearrange("(k p) -> p k", p=128), ["gvec"])
            ld(bgT[:], b_gate.rearrange("(k p) -> p k", p=128), ["bgT"])
            ld(lgT[:], sg_ln_g.rearrange("(k p) -> p k", p=128), ["lgT"])
            ld(lbT[:], sg_ln_b.rearrange("(k p) -> p k", p=128), ["lbT"])
            ld(SGt[:], subln_g.partition_broadcast(128), ["SGt"])
            ld(lam4, lamv.partition_broadcast(128), ["lam4", ("FA", 1)])
            ld(BR, braw.rearrange("h p n -> p h n"), ["BR", ("FA", 1)])
            ld(b31s[:], b31.rearrange("h p o -> p (h o)"), ["b31s"])
            ld(MK, cmask, ["MK", ("FA", 1)])
            ld(selt[:], sel_d, ["selt"])
        ph.op("pool", lambda e: e.memset(ones_b[:], 1.0), w=["ones_b"])
        ph.op("dve", lambda e: e.tensor_scalar(out=SGt[:], in0=SGt[:], scalar1=0.8, scalar2=None, op0=ALU.mult), r=["SGt"], w=["SGt"])
        ph.op("dve", lambda e: e.tensor_tensor(out=lam4[:, 0, :], in0=lam4[:, 0, :], in1=lam4[:, 1, :], op=ALU.mult), r=["lam4"], w=["lam4", ("FA", 1)])
        ph.op("dve", lambda e: e.tensor_tensor(out=lam4[:, 2, :], in0=lam4[:, 2, :], in1=lam4[:, 3, :], op=ALU.mult), r=["lam4"], w=["lam4", ("FA", 1)])
        ph.op("dve", lambda e: e.tensor_reduce(out=sm[:, 1:2], in_=lam4[:, 0, :], axis=mybir.AxisListType.X, op=ALU.add), r=["lam4", ("FA", 1)], w=["sm"])
        ph.op("dve", lambda e: e.tensor_reduce(out=sm[:, 2:3], in_=lam4[:, 2, :], axis=mybir.AxisListType.X, op=ALU.add), r=["lam4", ("FA", 1)], w=["sm"])
        ph.op("act", lambda e: e.activation(out=sm[:, 1:3], in_=sm[:, 1:3], func=AF.Exp), r=["sm"], w=["sm"])
        ph.op("dve", lambda e: e.tensor_tensor(out=sm[:, 0:1], in0=sm[:, 2:3], in1=sm[:, 1:2], op=ALU.subtract), r=["sm"], w=["sm"])
        ph.op("dve", lambda e: e.tensor_scalar(out=sm[:, 0:1], in0=sm[:, 0:1], scalar1=-0.2, scalar2=None, op0=ALU.add), r=["sm"], w=["sm"])
        for hl in range(2):
            ph.op("dve", lambda e, hl=hl: e.scalar_tensor_tensor(out=Bh[:, hl, :], in0=BR[:, hl, :], scalar=b31s[:, hl:hl + 1], in1=MK,
                                                                 op0=ALU.subtract, op1=ALU.add), r=["BR", "b31s", "MK", ("FA", 1)], w=["Bh"])
        def rstd_from_ss(ss_ap, n, rtoks):
            ph.op("dve", lambda e: e.tensor_scalar(out=ss_ap, in0=ss_ap, scalar1=1.0 / n, scalar2=EPS, op0=ALU.mult, op1=ALU.add), r=rtoks, w=rtoks)
            ph.op("act", lambda e: e.activation(out=ss_ap, in_=ss_ap, func=AF.Ln), r=rtoks, w=rtoks)
            ph.op("act", lambda e: e.activation(out=ss_ap, in_=ss_ap, func=AF.Exp, scale=-0.5), r=rtoks, w=rtoks)

        def norm_T(src, nrows, goff, dst):
            atv = AT[:, 0:KC * nrows].rearrange("p (k t) -> p k t", k=KC)
            for tb in range(nrows // 128):
                s = tb % 2
                ph.dma("sp", lambda e, tb=tb, s=s: e.dma_start(out=FA[:, s, 0:D], in_=src[tb * 128:(tb + 1) * 128, :]), w=[("FA", s)], sem="nl%d" % s)
                ssa = sm[:, 8 + s:9 + s]
                ph.op("act", lambda e, s=s, ssa=ssa: e.activation(out=JK[:, 0:D], in_=FA[:, s, 0:D], func=AF.Square, accum_out=ssa),
                      r=[("FA", s)], w=[("WB", 1), ("sm", 8 + s)])
                rstd_from_ss(ssa, D, [("sm", 8 + s)])
                ph.op("dve", lambda e, s=s, ssa=ssa: e.tensor_scalar(out=BA[:, s, 0:D], in0=FA[:, s, 0:D], scalar1=ssa, scalar2=None, op0=ALU.mult),
                      r=[("FA", s), ("sm", 8 + s)], w=[("BA", s)])
                for k4 in range(KC // 4):
                    bank = 4 + (k4 % 2)
                    ptb = PS[bank][:].bitcast(BF16)
                    for j in range(4):
                        kc = k4 * 4 + j
                        ph.op("pe", lambda e, s=s, kc=kc, j=j, ptb=ptb: e.transpose(ptb[:, j * 128:(j + 1) * 128], BA[:, s, kc * 128:(kc + 1) * 128], ident[:]),
                              r=[("BA", s), "ident"], w=[("PS", bank)])
                    for j in range(4):
                        kc = k4 * 4 + j
                        eng = "dve" if j % 2 == 0 else "act"
                        if eng == "dve":
                            ph.op("dve", lambda e, kc=kc, j=j, ptb=ptb, tb=tb: e.tensor_scalar(
                                out=atv[:, kc, tb * 128:(tb + 1) * 128], in0=ptb[:, j * 128:(j + 1) * 128], scalar1=gvec[:, goff + kc:goff + kc + 1],
                                scalar2=None, op0=ALU.mult), r=[("PS", bank), "gvec"], w=[("AT", kc)])
                        else:
                            ph.op("act", lambda e, kc=kc, j=j, ptb=ptb, tb=tb: e.activation(
                                out=atv[:, kc, tb * 128:(tb + 1) * 128], in_=ptb[:, j * 128:(j + 1) * 128], func=AF.Copy,
                                scale=gvec[:, goff + kc:goff + kc + 1]), r=[("PS", bank), "gvec"], w=[("AT", kc)])
            dv = dst.ap().rearrange("(k p) t -> p k t", p=128)
            for q4 in range(4):
                k0, k1 = q4 * KC // 4, (q4 + 1) * KC // 4
                ph.dma("pool", lambda e, k0=k0, k1=k1: e.dma_start(out=dv[:, k0:k1, :], in_=atv[:, k0:k1, :]),
                       r=[("AT", k) for k in range(k0, k1)], w=[(dst.name, q4)], sem="ns")

        def load_AT(src, r0, kcn, t0, nt, rtoks):
            atv = AT[:, 0:kcn * nt].rearrange("p (k t) -> p k t", k=kcn)
            sv = src.ap()[r0:r0 + kcn * 128, :].rearrange("(k p) t -> p k t", p=128)
            nq = 4 if kcn >= 4 else 1
            for q4 in range(nq):
                k0, k1 = q4 * kcn // nq, (q4 + 1) * kcn // nq
                ph.dma("sp", lambda e, k0=k0, k1=k1: e.dma_start(out=atv[:, k0:k1, :], in_=sv[:, k0:k1, t0:t0 + nt]),
                       r=rtoks, w=[("AT", k) for k in range(k0, k1)], sem="la")
            return atv

        def gemm(atv, kcn, nt, Wd, c0, ncols, mode, epi, wr):
            wv = Wd.ap().rearrange("(k p) n -> p k n", p=128)
            nch = ncols // 512
            attoks = [("AT", k) for k in range(kcn)]

            def wload(ci):
                s = cnt["ws"] = (cnt.get("ws", 0) + 1) % 2
                wbv = WB[:, s * 16384:s * 16384 + kcn * 512].rearrange("p (k n) -> p k n", k=kcn)
                nq = 4 if kcn >= 4 else 1
                for q4 in range(nq):
                    k0, k1 = q4 * kcn // nq, (q4 + 1) * kcn // nq
                    ph.dma("sp", lambda e, k0=k0, k1=k1, ci=ci: e.dma_start(out=wbv[:, k0:k1, :], in_=wv[:, k0:k1, c0 + ci * 512:c0 + (ci + 1) * 512]),
                           r=wr, w=[("WB", s)], sem="w%d" % s)
                return s, wbv

            nxt = wload(0)
            for ci in range(nch):
                s, wbv = nxt
                if ci + 1 < nch:
                    nxt = wload(ci + 1)
                if mode == "T":
                    for tb in range(nt // 128):
                        bank = cnt["ps"] = (cnt["ps"] + 1) % 4
                        ps = PS[bank]
                        for kc in range(kcn):
                            ph.op("pe", lambda e, ps=ps, kc=kc, tb=tb, wbv=wbv: e.matmul(ps[:, :], atv[:, kc, tb * 128:(tb + 1) * 128], wbv[:, kc, :],
                                                                                         start=(kc == 0), stop=(kc == kcn - 1)),
                                  r=[("WB", s)] + attoks, w=[("PS", bank)])
                        epi(ps, ("PS", bank), tb, ci)
                else:
                    th = min(512, nt)
                    for cs in range(4):
                        for hh in range(nt // th):
                            bank = cnt["ps"] = (cnt["ps"] + 1) % 4
                            ps = PS[bank]
                            for kc in range(kcn):
                                ph.op("pe", lambda e, ps=ps, kc=kc, hh=hh, cs=cs, wbv=wbv: e.matmul(
                                    ps[:, 0:th], wbv[:, kc, cs * 128:(cs + 1) * 128], atv[:, kc, hh * th:(hh + 1) * th],
                                    start=(kc == 0), stop=(kc == kcn - 1)), r=[("WB", s)] + attoks, w=[("PS", bank)])
                            epi(ps, ("PS", bank), ci * 4 + cs, hh)

        def et_slot():
            s = cnt["et"] = (cnt.get("et", 0) + 1) % 3
            return s

        def gelu_from(ps_ap, n, ptok, dst_ap, dtok, tmp_ap, ttok):
            ph.op("act", lambda e: e.activation(out=tmp_ap, in_=ps_ap, func=AF.Square), r=[ptok], w=[ttok])
            ph.op("dve", lambda e: e.tensor_scalar(out=tmp_ap, in0=tmp_ap, scalar1=0.044715, scalar2=1.0, op0=ALU.mult, op1=ALU.add), r=[ttok], w=[ttok])
            ph.op("dve", lambda e: e.tensor_tensor(out=tmp_ap, in0=tmp_ap, in1=ps_ap, op=ALU.mult), r=[ttok, ptok], w=[ttok])
            ph.op("act", lambda e: e.activation(out=tmp_ap, in_=tmp_ap, func=AF.Sigmoid, scale=1.5957691216057308), r=[ttok], w=[ttok])
            ph.op("dve", lambda e: e.tensor_tensor(out=dst_ap, in0=tmp_ap, in1=ps_ap, op=ALU.mult), r=[ttok, ptok], w=[dtok])

        def cast_to(src_ap, rows, cols, dstt, tag):
            sv = src_ap.rearrange("(a p) n -> p a n", p=min(128, rows)) if rows >= 128 else None
            dvv = dstt.ap().rearrange("(a p) n -> p a n", p=min(128, rows)) if rows >= 128 else None
            pr = min(128, rows)
            na = rows // pr
            cw = 4096
            for a in range(na):
                for c in range(0, cols, cw):
                    w_ = min(cw, cols - c)
                    s = cnt["cs"] = (cnt.get("cs", 0) + 1) % 2
                    if rows >= 128:
                        si, do = sv[:, a, c:c + w_], dvv[:, a, c:c + w_]
                    else:
                        si, do = src_ap[:, c:c + w_], dstt.ap()[:, c:c + w_]
                    ph.dma("sp", lambda e, si=si, s=s, w_=w_: e.dma_start(out=FA[0:pr, s, 0:w_], in_=si), w=[("FA", s)], sem="cl%d" % s)
                    ph.op("pool", lambda e, s=s, w_=w_: e.tensor_copy(out=BA[0:pr, s, 0:w_], in_=FA[0:pr, s, 0:w_]), r=[("FA", s)], w=[("BA", s)])
                    ph.dma("sp", lambda e, do=do, s=s, w_=w_: e.dma_start(out=do, in_=BA[0:pr, s, 0:w_]), r=[("BA", s)], w=[tag], sem="cst%d" % s)

        def allgather(src_t, dst_t, rtok, wtok_, sem):
            ph.dma("pool", lambda e: e.collective_compute("AllGather", ALU.bypass, replica_groups=[list(range(8))],
                                                          ins=[src_t.ap().opt()], outs=[dst_t.ap().opt()]),
                   r=[rtok], w=[wtok_], sem=sem, inc=1)

        norm_T(x, TOK, 0, hnT_own)
        ph.dma("pool", lambda e: e.collective_compute("AllGather", ALU.bypass, replica_groups=[list(range(8))],
                                                      ins=[hnT_own.ap().opt()], outs=[hnT_all.ap().opt()]),
               r=[("hnT_own", q) for q in range(4)], w=["hnT_all"], sem="cc0", inc=1)
        cast_to(wA, D, 2048, wA_b, "wA_b")
        for n, src, r, c in wspec:
            cast_to(src, r, c, Wsl[n], n + "_s")
            allgather(Wsl[n], Wf[n], n + "_s", n + "_f", "ccw")

        QTv, KTv = QT.ap(), KT.ap()
        for r8 in range(8):
            atv = load_AT(hnT_all, r8 * D, KC, 0, TOK, ["hnT_all"])
            tok0 = r8 * TOK

            def epi_qk(dstv):
                def f(ps, ptok, cc, hh):
                    s = et_slot()
                    ph.op("act" if cc % 2 == 0 else "dve",
                          (lambda e: e.activation(out=PT[:, s % 3, 0:TH], in_=ps[:, 0:TH], func=AF.Copy)) if cc % 2 == 0 else
                          (lambda e: e.tensor_copy(out=PT[:, s % 3, 0:TH], in_=ps[:, 0:TH])), r=[ptok], w=[("PT", s % 3)])
                    ph.dma("pool", lambda e: e.dma_start(out=dstv[cc, :, tok0 + hh * TH:tok0 + (hh + 1) * TH], in_=PT[:, s % 3, 0:TH]),
                           r=[("PT", s % 3)], w=[dstv.tensor.name], sem="st%d" % (s % 3))
                return f

            def epi_v(ps, ptok, tb, ci):
                s = et_slot()
                ph.op("act", lambda e: e.activation(out=PT[:, s % 3, :], in_=ps[:, :], func=AF.Copy), r=[ptok], w=[("PT", s % 3)])
                ph.dma("pool", lambda e: e.dma_start(out=Vd.ap()[:, tok0 + tb * 128:tok0 + (tb + 1) * 128, :].rearrange("h t d -> t h d"),
                                                     in_=PT[:, s % 3, :].rearrange("p (h d) -> p h d", h=2)),
                       r=[("PT", s % 3)], w=["Vd"], sem="st%d" % (s % 3))

            def epi_z(ps, ptok, tb, ci):
                s = et_slot()
                ph.op("act", lambda e: e.activation(out=EU[:, s, :], in_=ps[:, :], func=AF.Sigmoid), r=[ptok], w=[("EU", s)])
                ph.op("dve", lambda e: e.tensor_tensor(out=ET[:, s, :], in0=EU[:, s, :], in1=ps[:, :], op=ALU.mult), r=[("EU", s), ptok], w=[("ET", s)])
                ph.dma("pool", lambda e: e.dma_start(out=ZA.ap()[tok0 + tb * 128:tok0 + (tb + 1) * 128, :], in_=ET[:, s, :]),
                       r=[("ET", s)], w=["ZA"], sem="se%d" % s)

            gemm(atv, KC, TOK, wA_b, 0, 512, "F", epi_qk(QTv), ["wA_b"])
            gemm(atv, KC, TOK, wA_b, 512, 512, "F", epi_qk(KTv), ["wA_b"])
            gemm(atv, KC, TOK, wA_b, 1024, 512, "T", epi_v, ["wA_b"])
            gemm(atv, KC, TOK, wA_b, 1536, 512, "T", epi_z, ["wA_b"])

        scale = 128 ** -0.5
        KTs = AT[:, 0:2 * S].rearrange("p (c s) -> p c s", c=2)
        Vt = WB[:, 0:NB * 257].rearrange("p (b d) -> p b d", d=257)
        for hl in range(2):
            for c in range(2):
                ph.dma("sp", lambda e, c=c, hl=hl: e.dma_start(out=KTs[:, c, :], in_=KTv[hl * 2 + c, :, :]),
                       r=["KT"], w=[("AT", k) for k in range(KC)] + ["KTs"], sem="bk")
            nv = max(1, NB // 8)
            for b8 in range(0, NB, nv):
                ph.dma("sp", lambda e, b8=b8, hl=hl: e.dma_start(
                    out=Vt[:, b8:b8 + nv, 0:256], in_=Vd.ap()[hl, b8 * 128:(b8 + nv) * 128, :].rearrange("(b p) d -> p b d", p=128)),
                    r=["Vd"], w=[("WB", 0), ("WB", 1), "Vt"], sem="bv")
            ph.op("pool", lambda e: e.memset(Vt[:, :, 256:257], 1.0), r=[], w=[("WB", 0), ("WB", 1), "Vt"])
            for t in range(NQT):
                qs = t % 2
                ph.dma("sp", lambda e, t=t, qs=qs, hl=hl: e.dma_start(out=QTt[:, qs, :, :], in_=QTv[hl * 2:hl * 2 + 2, :, t * 256:(t + 1) * 256].rearrange("c p q -> p c q")),
                       r=["QT"], w=[("QTt", qs)], sem="bq%d" % qs)
                ph.dma("sp", lambda e, t=t, qs=qs, hl=hl: e.dma_start(out=EU[:, qs, :].rearrange("p (j d) -> p j d", j=2),
                                                                      in_=ZA.ap()[t * 256:(t + 1) * 256, hl * 256:(hl + 1) * 256].rearrange("(j p) d -> p j d", p=128)),
                       r=["ZA"], w=[("EU", qs)], sem="bz%d" % qs)
                nkb = 2 * t + 2
                Ob = [[PS[2], PS[3]], [PS[4], PS[5]]]
                for kb in range(nkb):
                    sbk = kb % 2
                    ps = PS[sbk]
                    psv = ps[:].rearrange("p (c q) -> p c q", c=2)
                    for c in range(2):
                        ph.op("pe", lambda e, c=c, kb=kb, psv=psv, qs=qs: e.matmul(psv[:, c, :], KTs[:, c, kb * 128:(kb + 1) * 128], QTt[:, qs, c, :], start=True, stop=True),
                              r=["KTs", ("QTt", qs)], w=[("PS", sbk)])
                    pslot = cnt["pt"] = (cnt.get("pt", 0) + 1) % 3
                    if kb >= nkb - 3:
                        off = {nkb - 1: 0, nkb - 2: 128, nkb - 3: 256}[kb]
                        es = kb % 2
                        ph.op("dve", lambda e, psv=psv, off=off, es=es, hl=hl: e.scalar_tensor_tensor(
                            out=ET[:, es, :].rearrange("p (c q) -> p c q", c=2), in0=psv, scalar=scale,
                            in1=Bh[:, hl, off:off + 256].unsqueeze(1).to_broadcast([128, 2, 256]), op0=ALU.mult, op1=ALU.add),
                            r=[("PS", sbk), "Bh"], w=[("ET", es)])
                        ph.op("act", lambda e, es=es, pslot=pslot: e.activation(out=PT[:, pslot, :], in_=ET[:, es, :], func=AF.Exp),
                              r=[("ET", es)], w=[("PT", pslot)])
                    else:
                        ph.op("act", lambda e, ps=ps, pslot=pslot: e.activation(out=PT[:, pslot, :], in_=ps[:, :], func=AF.Exp, scale=scale),
                              r=[("PS", sbk)], w=[("PT", pslot)])
                    for c in range(2):
                        for j in range(2):
                            ph.op("pe", lambda e, c=c, j=j, kb=kb, pslot=pslot: e.matmul(
                                Ob[c][j][:, 0:257], PT[:, pslot, c * 256 + j * 128:c * 256 + (j + 1) * 128], Vt[:, kb, :],
                                start=(kb == 0), stop=(kb == nkb - 1)), r=[("PT", pslot), "Vt"], w=[("PS", 2 + c * 2 + j)])
                gs = 0
                for j in range(2):
                    for c in range(2):
                        ph.op("act" if c == 0 else "dve",
                              (lambda e, c=c, j=j: e.activation(out=OS[:, c * 2 + j, 0:257], in_=Ob[c][j][:, 0:257], func=AF.Copy)) if c == 0 else
                              (lambda e, c=c, j=j: e.tensor_copy(out=OS[:, c * 2 + j, 0:257], in_=Ob[c][j][:, 0:257])),
                              r=[("PS", 2 + c * 2 + j)], w=[("OS", c * 2 + j)])
                for j in range(2):
                    o1, o2 = OS[:, j, :], OS[:, 2 + j, :]
                    rt = [("OS", j), ("OS", 2 + j)]
                    sj = sm[:, 16 + 4 * j:20 + 4 * j]
                    stok = ("sm", 16 + j)
                    ph.op("dve", lambda e, o1=o1, sj=sj: e.reciprocal(out=sj[:, 0:1], in_=o1[:, 256:257]), r=rt, w=[stok])
                    ph.op("dve", lambda e, o2=o2, sj=sj: e.reciprocal(out=sj[:, 1:2], in_=o2[:, 256:257]), r=rt, w=[stok])
                    ph.op("dve", lambda e, sj=sj: e.tensor_tensor(out=sj[:, 1:2], in0=sj[:, 1:2], in1=sm[:, 0:1], op=ALU.mult), r=[stok, "sm"], w=[stok])
                    ph.op("dve", lambda e, o1=o1, sj=sj: e.tensor_scalar(out=o1[:, 0:256], in0=o1[:, 0:256], scalar1=sj[:, 0:1], scalar2=None, op0=ALU.mult), r=rt + [stok], w=rt)
                    ph.op("dve", lambda e, o1=o1, o2=o2, sj=sj: e.scalar_tensor_tensor(out=o1[:, 0:256], in0=o2[:, 0:256], scalar=sj[:, 1:2], in1=o1[:, 0:256],
                                                                                     op0=ALU.mult, op1=ALU.add), r=rt + [stok], w=rt)
                    ph.op("act", lambda e, o1=o1, o2=o2, sj=sj: e.activation(out=o2[:, 0:256], in_=o1[:, 0:256], func=AF.Square, accum_out=sj[:, 2:3]), r=rt, w=rt + [stok])
                    rstd_from_ss(sj[:, 2:3], 256, [stok])
                    ph.op("dve", lambda e, o1=o1, sj=sj: e.scalar_tensor_tensor(out=o1[:, 0:256], in0=o1[:, 0:256], scalar=sj[:, 2:3], in1=SGt[:],
                                                                              op0=ALU.mult, op1=ALU.mult), r=rt + [stok, "SGt"], w=rt)
                    ph.op("dve", lambda e, o1=o1, j=j, qs=qs: e.tensor_tensor(out=BA[:, 1, j * 256:(j + 1) * 256], in0=o1[:, 0:256],
                                                                               in1=EU[:, qs, j * 256:(j + 1) * 256], op=ALU.mult),
                          r=rt + [("EU", qs)], w=[("BA", 1)])
                ptb = PS[6][:].bitcast(BF16)
                for j in range(2):
                    for dc in range(2):
                        ph.op("pe", lambda e, j=j, dc=dc, ptb=ptb: e.transpose(ptb[:, dc * 256 + j * 128:dc * 256 + (j + 1) * 128],
                                                                               BA[:, 1, j * 256 + dc * 128:j * 256 + (dc + 1) * 128], ident[:]),
                              r=[("BA", 1), "ident"], w=[("PS", 6)])
                ph.op("act", lambda e, gs=gs, ptb=ptb: e.activation(out=GAs[:, gs, :, :].rearrange("p a q -> p (a q)"), in_=ptb[:, 0:512], func=AF.Copy),
                      r=[("PS", 6)], w=[("GAs", gs)])
                ph.dma("pool", lambda e, gs=gs, t=t, hl=hl: e.dma_start(
                    out=gaT_own.ap()[hl * 256:(hl + 1) * 256, t * 256:(t + 1) * 256].rearrange("(a p) q -> p a q", p=128), in_=GAs[:, gs, :, :]),
                    r=[("GAs", gs)], w=["gaT_own"], sem="sg%d" % gs)
        allgather(gaT_own, gaT_all, "gaT_own", "gaT_all", "cc1")

        atv = load_AT(hnT_own, 0, KC, 0, TOK, [("hnT_own", q) for q in range(4)])
        usv, gTv = usT.ap(), gT.ap()

        def epi_u(ps, ptok, cc, hh):
            s = et_slot()
            gelu_from(ps[:, 0:TH], TH, ptok, ET[:, s, 0:TH], ("ET", s), EU[:, s, 0:TH], ("EU", s))
            ph.dma("pool", lambda e: e.dma_start(out=usv[cc * 128:(cc + 1) * 128, hh * TH:(hh + 1) * TH], in_=ET[:, s, 0:TH]),
                   r=[("ET", s)], w=["usT"], sem="se%d" % s)

        def epi_zb(ps, ptok, cc, hh):
            s = et_slot()
            ph.dma("sp", lambda e: e.dma_start(out=ET[:, s, 0:TH], in_=usv[cc * 128:(cc + 1) * 128, hh * TH:(hh + 1) * TH]),
                   r=["usT"], w=[("ET", s)], sem="le%d" % s)
            ph.op("act", lambda e: e.activation(out=EU[:, s, 0:TH], in_=ps[:, 0:TH], func=AF.Sigmoid), r=[ptok], w=[("EU", s)])
            ph.op("dve", lambda e: e.tensor_tensor(out=EU[:, s, 0:TH], in0=EU[:, s, 0:TH], in1=ps[:, 0:TH], op=ALU.mult), r=[("EU", s), ptok], w=[("EU", s)])
            ph.op("dve", lambda e: e.tensor_tensor(out=ET[:, s, 0:TH], in0=ET[:, s, 0:TH], in1=EU[:, s, 0:TH], op=ALU.mult), r=[("EU", s), ("ET", s)], w=[("ET", s)])
            ph.dma("pool", lambda e: e.dma_start(out=usv[cc * 128:(cc + 1) * 128, hh * TH:(hh + 1) * TH], in_=ET[:, s, 0:TH]),
                   r=[("ET", s)], w=["usT2"], sem="se%d" % s)

        def epi_vb(ps, ptok, tb, ci):
            s = et_slot()
            gelu_from(ps[:, :], 512, ptok, ET[:, s, :], ("ET", s), EU[:, s, :], ("EU", s))
            ph.dma("pool", lambda e: e.dma_start(out=Vg.ap()[tb * 128:(tb + 1) * 128, ci * 512:(ci + 1) * 512], in_=ET[:, s, :]),
                   r=[("ET", s)], w=["Vg"], sem="se%d" % s)

        def epi_g(base):
            def f(ps, ptok, cc, hh):
                s = et_slot()
                ph.op("act", lambda e: e.activation(out=ET[:, s, 0:TH], in_=ps[:, 0:TH], func=AF.Sigmoid, bias=bgT[:, base + cc:base + cc + 1]),
                      r=[ptok, "bgT"], w=[("ET", s)])
                ph.dma("pool", lambda e: e.dma_start(out=gTv[(base + cc) * 128:(base + cc + 1) * 128, hh * TH:(hh + 1) * TH], in_=ET[:, s, 0:TH]),
                       r=[("ET", s)], w=["gT"], sem="se%d" % s)
            return f

        gemm(atv, KC, TOK, Wf["Wu"], 0, 4096, "F", epi_u, ["Wu_f"])
        gemm(atv, KC, TOK, Wf["Wz"], 0, 4096, "F", epi_zb, ["Wz_f"])
        gemm(atv, KC, TOK, Wf["Wv"], 0, 4096, "T", epi_vb, ["Wv_f"])
        gemm(atv, KC, TOK, Wf["Wga"], 0, D, "F", epi_g(0), ["Wga_f"])
        gemm(atv, KC, TOK, Wf["Wgb"], 0, D, "F", epi_g(KC), ["Wgb_f"])

        with nc.allow_non_contiguous_dma(reason="sg_w"):
            ld(FA[:, 0, 0:2048].rearrange("p (g s) -> p g s", g=16), sg_w.rearrange("g t s -> t g s"), [("FA", 0)], sem="su2")
            ld(FA[:, 1, 0:128], tril, [("FA", 1)], sem="su4")
        for gi in range(16):
            ph.op("dve", lambda e, gi=gi: e.tensor_tensor(out=BA[:, 0, gi * 128:(gi + 1) * 128], in0=FA[:, 0, gi * 128:(gi + 1) * 128],
                                                          in1=FA[:, 1, 0:128], op=ALU.mult), r=[("FA", 0), ("FA", 1)], w=[("BA", 0)])
        for gi in range(16):
            pt = PS[gi % 2]
            ptb = pt[:].bitcast(BF16)
            ph.op("pe", lambda e, gi=gi, ptb=ptb: e.transpose(ptb[:, 0:128], BA[:, 0, gi * 128:(gi + 1) * 128], ident[:]),
                  r=[("BA", 0), "ident"], w=[("PS", gi % 2)])
            ph.op("act", lambda e, gi=gi, ptb=ptb: e.activation(out=wsT[:, gi, :], in_=ptb[:, 0:128], func=AF.Copy), r=[("PS", gi % 2)], w=["wsT", ("WB", 0)])
        with nc.allow_non_contiguous_dma(reason="broadcast"):
            ph.dma("sp", lambda e: e.dma_start(out=SBv, in_=sg_b.partition_broadcast(128)), w=[("WB", 0)], sem="su5")
        for gi in range(16):
            pt = PS[2 + gi % 2]
            ph.op("pe", lambda e, gi=gi, pt=pt: e.matmul(pt[:, 0:128], ones_b[:], wsT[:, gi, :], start=True, stop=True),
                  r=["ones_b", "wsT", ("WB", 0)], w=[("PS", 2 + gi % 2)])
            for dc in range(2):
                ph.op("dve", lambda e, gi=gi, dc=dc, pt=pt: e.scalar_tensor_tensor(
                    out=CB[:, gi * 2 + dc, :], in0=pt[:, 0:128], scalar=lbT[:, gi * 2 + dc:gi * 2 + dc + 1], in1=SBv[:, gi * 128:(gi + 1) * 128],
                    op0=ALU.mult, op1=ALU.add), r=[("PS", 2 + gi % 2), "lbT", ("WB", 0)], w=["CB", ("WB", 0)])

        vh = AT[:, 0:NTB * 4096].rearrange("p (b n) -> p b n", b=NTB)
        for tb in range(NTB):
            s = tb % 2
            ph.dma("sp", lambda e, tb=tb, s=s: e.dma_start(out=FA[:, s, :], in_=Vg.ap()[tb * 128:(tb + 1) * 128, :]), r=["Vg"], w=[("FA", s)], sem="nl%d" % s)
            sa = sm[:, 24 + 4 * s:28 + 4 * s]
            stok = ("sm", 24 + s)
            ph.op("act", lambda e, s=s, sa=sa: e.activation(out=JK, in_=FA[:, s, :], func=AF.Identity, accum_out=sa[:, 0:1]), r=[("FA", s)], w=[("WB", 1), stok])
            ph.op("act", lambda e, s=s, sa=sa: e.activation(out=JK, in_=FA[:, s, :], func=AF.Square, accum_out=sa[:, 1:2]), r=[("FA", s)], w=[("WB", 1), stok])
            ph.op("dve", lambda e, sa=sa: e.tensor_scalar(out=sa[:, 0:2], in0=sa[:, 0:2], scalar1=1.0 / 4096, scalar2=None, op0=ALU.mult), r=[stok], w=[stok])
            ph.op("dve", lambda e, sa=sa: e.tensor_tensor(out=sa[:, 2:3], in0=sa[:, 0:1], in1=sa[:, 0:1], op=ALU.mult), r=[stok], w=[stok])
            ph.op("dve", lambda e, sa=sa: e.tensor_tensor(out=sa[:, 1:2], in0=sa[:, 1:2], in1=sa[:, 2:3], op=ALU.subtract), r=[stok], w=[stok])
            ph.op("dve", lambda e, sa=sa: e.tensor_scalar(out=sa[:, 1:2], in0=sa[:, 1:2], scalar1=EPS, scalar2=None, op0=ALU.add), r=[stok], w=[stok])
            ph.op("act", lambda e, sa=sa: e.activation(out=sa[:, 1:2], in_=sa[:, 1:2], func=AF.Ln), r=[stok], w=[stok])
            ph.op("act", lambda e, sa=sa: e.activation(out=sa[:, 1:2], in_=sa[:, 1:2], func=AF.Exp, scale=-0.5), r=[stok], w=[stok])
            ph.op("dve", lambda e, s=s, sa=sa, tb=tb: e.tensor_scalar(out=vh[:, tb, :], in0=FA[:, s, :], scalar1=sa[:, 0:1], scalar2=sa[:, 1:2],
                                                                     op0=ALU.subtract, op1=ALU.mult), r=[("FA", s), stok], w=[("AT", k) for k in range(32)])
        TQ = min(4, NTB)
        for gi in range(16):
            for dc in range(2):
                cc = gi * 2 + dc
                for tq in range(NTB // TQ):
                    bank = cnt["ps"] = (cnt["ps"] + 1) % 4
                    ps = PS[bank]
                    for i in range(TQ):
                        tb = tq * TQ + i
                        ph.op("pe", lambda e, ps=ps, i=i, tb=tb, cc=cc, gi=gi: e.matmul(ps[:, i * 128:(i + 1) * 128], vh[:, tb, cc * 128:(cc + 1) * 128], wsT[:, gi, :],
                                                                                         start=True, stop=True),
                              r=[("AT", k) for k in range(32)] + ["wsT", ("WB", 0)], w=[("PS", bank)])
                    s = et_slot()
                    W_ = TQ * 128
                    ph.dma("sp", lambda e, s=s, cc=cc, tq=tq, W_=W_: e.dma_start(out=EU[:, s, 0:W_], in_=usv[cc * 128:(cc + 1) * 128, tq * W_:(tq + 1) * W_]),
                           r=["usT2"], w=[("EU", s)], sem="le%d" % s)
                    ph.op("dve", lambda e, ps=ps, s=s, cc=cc, W_=W_: e.scalar_tensor_tensor(
                        out=ET[:, s, 0:W_].rearrange("p (i t) -> p i t", i=TQ), in0=ps[:, 0:W_].rearrange("p (i t) -> p i t", i=TQ),
                        scalar=lgT[:, cc:cc + 1], in1=CB[:, cc, :].unsqueeze(1).to_broadcast([128, TQ, 128]), op0=ALU.mult, op1=ALU.add),
                        r=[("PS", bank), "lgT", "CB", ("WB", 0)], w=[("ET", s)])
                    ph.op("dve", lambda e, s=s, W_=W_: e.tensor_tensor(out=PT[:, s % 3, 0:W_], in0=ET[:, s, 0:W_], in1=EU[:, s, 0:W_], op=ALU.mult),
                          r=[("ET", s), ("EU", s)], w=[("PT", s % 3)])
                    ph.dma("pool", lambda e, s=s, cc=cc, tq=tq, W_=W_: e.dma_start(out=gbT.ap()[cc * 128:(cc + 1) * 128, tq * W_:(tq + 1) * W_], in_=PT[:, s % 3, 0:W_]),
                           r=[("PT", s % 3)], w=["gbT"], sem="st%d" % (s % 3))

        mTv = mT.ap()

        def epi_pb(ps, ptok, cc, hh):
            s = et_slot()
            ph.dma("sp", lambda e: e.dma_start(out=EU[:, s, 0:TH], in_=gTv[(KC + cc) * 128:(KC + cc + 1) * 128, hh * TH:(hh + 1) * TH]),
                   r=["gT"], w=[("EU", s)], sem="le%d" % s)
            ph.op("dve", lambda e: e.tensor_tensor(out=ET[:, s, 0:TH], in0=EU[:, s, 0:TH], in1=ps[:, 0:TH], op=ALU.mult), r=[("EU", s), ptok], w=[("ET", s)])
            ph.dma("pool", lambda e: e.dma_start(out=mTv[cc * 128:(cc + 1) * 128, hh * TH:(hh + 1) * TH], in_=ET[:, s, 0:TH]),
                   r=[("ET", s)], w=["mT"], sem="se%d" % s)

        def epi_pa(ps, ptok, cc, hh):
            s = et_slot()
            ph.dma("sp", lambda e: e.dma_start(out=EU[:, s, 0:TH], in_=gTv[cc * 128:(cc + 1) * 128, hh * TH:(hh + 1) * TH]),
                   r=["gT"], w=[("EU", s)], sem="le%d" % s)
            ph.dma("sp", lambda e: e.dma_start(out=ET[:, s, 0:TH], in_=mTv[cc * 128:(cc + 1) * 128, hh * TH:(hh + 1) * TH]),
                   r=["mT"], w=[("ET", s)], sem="lf%d" % s)
            ph.op("dve", lambda e: e.tensor_tensor(out=EU[:, s, 0:TH], in0=EU[:, s, 0:TH], in1=ps[:, 0:TH], op=ALU.mult), r=[("EU", s), ptok], w=[("EU", s)])
            ph.op("dve", lambda e: e.tensor_tensor(out=PT[:, s % 3, 0:TH], in0=EU[:, s, 0:TH], in1=ET[:, s, 0:TH], op=ALU.add), r=[("EU", s), ("ET", s)], w=[("PT", s % 3)])
            ph.dma("pool", lambda e: e.dma_start(out=mTb.ap()[cc * 128:(cc + 1) * 128, hh * TH:(hh + 1) * TH], in_=PT[:, s % 3, 0:TH]),
                   r=[("PT", s % 3)], w=["mTb"], sem="st%d" % (s % 3))

        def epi_out(ps, ptok, tb, ci):
            s = et_slot()
            ph.dma("sp", lambda e: e.dma_start(out=EU[:, s, :], in_=x[tb * 128:(tb + 1) * 128, ci * 512:(ci + 1) * 512]), w=[("EU", s)], sem="le%d" % s)
            ph.op("dve", lambda e: e.tensor_tensor(out=ET[:, s, :], in0=EU[:, s, :], in1=ps[:, :], op=ALU.add), r=[("EU", s), ptok], w=[("ET", s)])
            ph.dma("pool", lambda e: e.dma_start(out=h1.ap()[tb * 128:(tb + 1) * 128, ci * 512:(ci + 1) * 512], in_=ET[:, s, :]),
                   r=[("ET", s)], w=["h1"], sem="se%d" % s)

        atv = load_AT(gbT, 0, 32, 0, TOK, ["gbT"])
        gemm(atv, 32, TOK, Wf["Wpb"], 0, D, "F", epi_pb, ["Wpb_f"])

        atv = AT[:, 0:32 * TOK].rearrange("p (k t) -> p k t", k=32)
        gav = gaT_all.ap().rearrange("(k p) s -> p k s", p=128)
        for r8 in range(8):
            for hf in range(2):
                wbv = WB[:, hf * 16384:hf * 16384 + 16 * TOK].rearrange("p (k t) -> p k t", k=16)
                for q2 in range(2):
                    ph.dma("sp", lambda e, r8=r8, hf=hf, q2=q2, wbv=wbv: e.dma_start(
                        out=wbv[:, q2 * 8:(q2 + 1) * 8, :], in_=gav[:, hf * 16 + q2 * 8:hf * 16 + (q2 + 1) * 8, r8 * TOK:(r8 + 1) * TOK]),
                        r=["gaT_all"], w=[("WB", hf)], sem="w%d" % hf)
                ats = [("AT", k) for k in range(hf * 16, hf * 16 + 16)]
                if r8 == 0:
                    ph.op("dve", lambda e, hf=hf, wbv=wbv: e.tensor_scalar(out=atv[:, hf * 16:(hf + 1) * 16, :], in0=wbv, scalar1=selt[:, 0:1], scalar2=None, op0=ALU.mult),
                          r=[("WB", hf), "selt"], w=ats)
                else:
                    ph.op("dve", lambda e, hf=hf, wbv=wbv, r8=r8: e.scalar_tensor_tensor(out=atv[:, hf * 16:(hf + 1) * 16, :], in0=wbv, scalar=selt[:, r8:r8 + 1],
                                                                                       in1=atv[:, hf * 16:(hf + 1) * 16, :], op0=ALU.mult, op1=ALU.add),
                          r=[("WB", hf), "selt"] + ats, w=ats)
        gemm(atv, 32, TOK, Wf["Wpa"], 0, D, "F", epi_pa, ["Wpa_f"])
        atv = load_AT(mTb, 0, KC, 0, TOK, ["mTb"])
        gemm(atv, KC, TOK, Wf["Wout"], 0, D, "T", epi_out, ["Wout_f"])

        norm_T(mem, 256, 2 * KC, mnT)
        atm = AT[:, 0:KC * 256].rearrange("p (k t) -> p k t", k=KC)

        def epi_kx(ps, ptok, cc, hh):
            ph.op("act", lambda e: e.activation(out=KxT[:, cc, :], in_=ps[:, 0:256], func=AF.Copy), r=[ptok], w=["KxT"] + EUall)

        def epi_vx(ps, ptok, tb, ci):
            ph.op("act", lambda e: e.activation(out=Vx[:, tb, :, 0:128], in_=ps[:, :].rearrange("p (h d) -> p h d", h=4), func=AF.Copy), r=[ptok], w=["Vx"] + OSall)

        ph.op("pool", lambda e: e.memset(Vx[:, :, :, 128:132], 1.0), w=["Vx"] + OSall)
        gemm(atm, KC, 256, Wf["Wxkv"], 0, 512, "F", epi_kx, ["Wxkv_f"])
        gemm(atm, KC, 256, Wf["Wxkv"], 512, 512, "T", epi_vx, ["Wxkv_f"])
        norm_T(h1, TOK, KC, hn2T)
        atv = AT[:, 0:KC * TOK].rearrange("p (k t) -> p k t", k=KC)

        def epi_qx(ps, ptok, cc, hh):
            ph.op("act", lambda e: e.activation(out=qxT[:, cc, hh * TH:(hh + 1) * TH], in_=ps[:, 0:TH], func=AF.Copy), r=[ptok], w=["qxT"] + ETall)

        gemm(atv, KC, TOK, Wf["Wxq"], 0, 512, "F", epi_qx, ["Wxq_f"])
        oxT = AT[:, 0:4 * TOK].rearrange("p (k t) -> p k t", k=4)
        for hh in range(NTH):
            for h in range(4):
                pts = []
                for kb in range(2):
                    ps = PS[kb]
                    ph.op("pe", lambda e, ps=ps, kb=kb, h=h, hh=hh: e.matmul(ps[:, 0:TH], KxT[:, h, kb * 128:(kb + 1) * 128], qxT[:, h, hh * TH:(hh + 1) * TH], start=True, stop=True),
                          r=["KxT", "qxT"], w=[("PS", kb)])
                    pslot = cnt["pt"] = (cnt.get("pt", 0) + 1) % 3
                    ph.op("act", lambda e, ps=ps, pslot=pslot: e.activation(out=PT[:, pslot, 0:TH], in_=ps[:, 0:TH], func=AF.Exp, scale=scale), r=[("PS", kb)], w=[("PT", pslot)])
                    pts.append(pslot)
                for j in range(TH // 128):
                    po = PS[2 + j]
                    for kb in range(2):
                        ph.op("pe", lambda e, po=po, j=j, kb=kb, h=h, pts=pts: e.matmul(po[:, 0:129], PT[:, pts[kb], j * 128:(j + 1) * 128], Vx[:, kb, h, 0:129],
                                                                                      start=(kb == 0), stop=(kb == 1)), r=[("PT", pts[kb]), "Vx"], w=[("PS", 2 + j)])
                    sj = sm[:, 40 + j:41 + j]
                    ph.op("dve", lambda e, po=po, sj=sj: e.reciprocal(out=sj, in_=po[:, 128:129]), r=[("PS", 2 + j)], w=[("sm", 40 + j)])
                    ph.op("dve", lambda e, po=po, sj=sj, j=j, h=h: e.tensor_scalar(out=BA[:, 0, j * 512 + h * 128:j * 512 + (h + 1) * 128], in0=po[:, 0:128], scalar1=sj, scalar2=None, op0=ALU.mult),
                          r=[("PS", 2 + j), ("sm", 40 + j)], w=[("BA", 0)])
            for j in range(TH // 128):
                ptb = PS[6 + j % 2][:].bitcast(BF16)
                for h in range(4):
                    ph.op("pe", lambda e, j=j, h=h, ptb=ptb: e.transpose(ptb[:, h * 128:(h + 1) * 128], BA[:, 0, j * 512 + h * 128:j * 512 + (h + 1) * 128], ident[:]),
                          r=[("BA", 0), "ident"], w=[("PS", 6 + j % 2)])
                tk0 = hh * TH + j * 128
                ph.op("act", lambda e, ptb=ptb, tk0=tk0: e.activation(out=oxT[:, :, tk0:tk0 + 128], in_=ptb[:, 0:512].rearrange("p (h t) -> p h t", h=4), func=AF.Copy),
                      r=[("PS", 6 + j % 2)], w=[("AT", k) for k in range(KC)])
        wxo_sb = WB[:, 0:4 * D].rearrange("p (k n) -> p k n", k=4)
        ph.dma("sp", lambda e: e.dma_start(out=wxo_sb, in_=Wf["Wxo"].ap().rearrange("(k p) n -> p k n", p=128)), r=["Wxo_f"], w=[("WB", 0)], sem="w0")
        FGv = WB[:, 16384:16384 + 2 * D].bitcast(F32)
        with nc.allow_non_contiguous_dma(reason="broadcast"):
            ph.dma("sp", lambda e: e.dma_start(out=FGv, in_=final_g.partition_broadcast(128)), w=[("WB", 1)], sem="su3")
        for tb in range(NTB):
            s = tb % 2
            ph.dma("sp", lambda e, tb=tb, s=s: e.dma_start(out=FA[:, s, 0:D], in_=h1.ap()[tb * 128:(tb + 1) * 128, :]), r=["h1"], w=[("FA", s)], sem="nl%d" % s)
            for ci in range(D // 512):
                bank = cnt["ps"] = (cnt["ps"] + 1) % 4
                ps = PS[bank]
                for kc in range(4):
                    ph.op("pe", lambda e, ps=ps, kc=kc, tb=tb, ci=ci: e.matmul(ps[:, :], oxT[:, kc, tb * 128:(tb + 1) * 128], wxo_sb[:, kc, ci * 512:(ci + 1) * 512],
                                                                                start=(kc == 0), stop=(kc == 3)),
                          r=[("AT", k) for k in range(KC)] + [("WB", 0)], w=[("PS", bank)])
                ph.op("dve", lambda e, ps=ps, s=s, ci=ci: e.tensor_tensor(out=FA[:, s, ci * 512:(ci + 1) * 512], in0=FA[:, s, ci * 512:(ci + 1) * 512], in1=ps[:, :], op=ALU.add),
                      r=[("PS", bank), ("FA", s)], w=[("FA", s)])
            ssa = sm[:, 48 + s:49 + s]
            fo = AT[:, 16384 + s * 8192:16384 + s * 8192 + 2 * D].bitcast(F32)
            ftok = [("AT", 16 + s * 8 + k) for k in range(8)]
            ph.op("act", lambda e, s=s, ssa=ssa, fo=fo: e.activation(out=fo, in_=FA[:, s, 0:D], func=AF.Square, accum_out=ssa), r=[("FA", s)], w=ftok + [("sm", 48 + s)])
            rstd_from_ss(ssa, D, [("sm", 48 + s)])
            ph.op("dve", lambda e, s=s, ssa=ssa, fo=fo: e.scalar_tensor_tensor(out=fo, in0=FA[:, s, 0:D], scalar=ssa, in1=FGv, op0=ALU.mult, op1=ALU.mult),
                  r=[("FA", s), ("sm", 48 + s), ("WB", 1)], w=ftok)
            ph.dma("pool", lambda e, tb=tb, fo=fo: e.dma_start(out=out[tb * 128:(tb + 1) * 128, :], in_=fo), r=ftok, w=["out"], sem="so")
        ph.op("pool", lambda e: e.memset(sm[:, 60:61], 0.0), r=["out"], w=[("sm", 60)])
        with nc.allow_non_contiguous_dma(reason="small strided constant loads / layout stores"):
            ph.emit()
    return nc


def make_in_maps(D, S, x, mem, w_in, b_gate, norm1_g, lam_q1, lam_k1, lam_q2, lam_k2, subln_g, rel_bias, sg_ln_g, sg_ln_b,
                 sg_w, sg_b, w_proj_a, w_proj_b, w_out, norm_x_g, norm_mem_g, w_xq, w_xkv, w_xo, final_g):
    f = np.float32
    TOK = S // 8
    x2 = np.asarray(x, f).reshape(S, D)
    mem2 = np.ascontiguousarray(np.asarray(mem, f).reshape(256, D))
    w_in = np.asarray(w_in, f)[0]
    rel_bias = np.asarray(rel_bias, f)
    kk = np.arange(128)[:, None]
    qq = np.arange(128)[None, :]
    b0 = t5_bucket_np(qq - kk)
    b1 = t5_bucket_np(128 + qq - kk)
    cm = np.zeros((128, 512), f)
    cm[:, 0:128] = NEG
    cm[:, 128:256] = np.where(qq - kk >= 0, 0.0, NEG)
    ident = np.eye(128, dtype=f).astype(ml_dtypes.bfloat16)
    tril = np.tril(np.ones((128, 128), f))
    lamv = np.stack([np.asarray(v, f)[0] for v in (lam_q1, lam_k1, lam_q2, lam_k2)], 0)
    wpa = np.asarray(w_proj_a, f)[0]
    wpb = np.asarray(w_proj_b, f)[0]
    wo = np.asarray(w_out, f)[0]
    wq = np.asarray(w_xq, f)[0]
    wkv = np.asarray(w_xkv, f)[0]
    wxo = np.asarray(w_xo, f)[0]
    maps = []
    for c in range(8):
        hs = [2 * c, 2 * c + 1]
        cols = []
        for base in (0, 4096, 8192, 12288):
            for h in hs:
                cols.append(np.arange(base + h * 256, base + (h + 1) * 256))
        cols = np.concatenate(cols)
        braw = np.empty((2, 128, 512), f)
        b31 = np.empty((2, 128, 1), f)
        for i, h in enumerate(hs):
            braw[i, :, 0:128] = rel_bias[31, h]
            braw[i, :, 128:256] = rel_bias[b0, h]
            braw[i, :, 256:384] = rel_bias[b1, h]
            braw[i, :, 384:512] = rel_bias[31, h]
            b31[i, :, 0] = rel_bias[31, h]
        sel = np.zeros((128, 8), f)
        sel[:, c] = 1.0
        r0, r1 = c * D // 8, (c + 1) * D // 8
        maps.append({
            "x": np.ascontiguousarray(x2[c * TOK:(c + 1) * TOK]),
            "mem": mem2,
            "wA": np.ascontiguousarray(w_in[:, cols]),
            "wtok": np.ascontiguousarray(w_in[r0:r1, 16384:]),
            "wpa": np.ascontiguousarray(wpa[c * 512:(c + 1) * 512]),
            "wpb": np.ascontiguousarray(wpb[c * 512:(c + 1) * 512]),
            "wout": np.ascontiguousarray(wo[r0:r1]),
            "wxq": np.ascontiguousarray(wq[r0:r1]),
            "wxkv": np.ascontiguousarray(wkv[r0:r1]),
            "wxo": np.ascontiguousarray(wxo[c * 64:(c + 1) * 64]),
            "norm1_g": np.asarray(norm1_g, f)[0], "norm_x_g": np.asarray(norm_x_g, f)[0], "norm_mem_g": np.asarray(norm_mem_g, f)[0],
            "final_g": np.asarray(final_g, f), "b_gate": np.asarray(b_gate, f)[0], "subln_g": np.asarray(subln_g, f)[0],
            "sg_ln_g": np.asarray(sg_ln_g, f)[0], "sg_ln_b": np.asarray(sg_ln_b, f)[0],
            "sg_w": np.ascontiguousarray(np.asarray(sg_w, f)[0]), "sg_b": np.ascontiguousarray(np.asarray(sg_b, f)[0].reshape(-1)),
            "lamv": lamv, "braw": braw, "b31": b31, "cmask": cm, "tril": tril, "ident": ident, "sel": sel,
        })
    return maps


_NC_CACHE = {}


def run_kernel(D, S, inputs, trace=False):
    key = (D, S)
    if key not in _NC_CACHE:
        _NC_CACHE[key] = build(D, S)
    nc = _NC_CACHE[key]
    inputs = dict(inputs)
    inputs.pop("positions", None)
    maps = make_in_maps(D, S, **inputs)
    res = run_bass_kernel_spmd(nc, maps, core_ids=list(range(8)), trace=trace)
    outp = np.concatenate([r["out"] for r in res.results], axis=0).reshape(1, S, D).astype(np.float32)
    return outp, res


def kernel(**inputs):
    outp, _ = run_kernel(4096, 8192, inputs)
    return outp
```

```python
import math
import types
from contextlib import ExitStack
import numpy as np
import ml_dtypes
import concourse.bass as bass
import concourse.mybir as mybir
from concourse.bass_utils import run_bass_kernel_spmd

F32 = mybir.dt.float32
BF16 = mybir.dt.bfloat16
AF = mybir.ActivationFunctionType
ALU = mybir.AluOpType

ENGS = ["pe", "act", "dve", "pool", "sp"]
NEG = -30000.0
EPS = 1e-6


class G:
    def __init__(self, nc, stack, n_dma_sems=48):
        self.nc = nc
        self.esem = {e: stack.enter_context(nc.semaphore("s_" + e)) for e in ENGS}
        self.ecount = {e: 0 for e in ENGS}
        self.dsem_pool = [stack.enter_context(nc.semaphore("d%d" % i)) for i in range(n_dma_sems)]
        self.dsem = {}
        self.dcount = {}
        self.frontier = {}

    def get_dsem(self, name):
        if name not in self.dsem:
            self.dsem[name] = self.dsem_pool[len(self.dsem)]
            self.dcount[name] = 0
        return self.dsem[name]


def _snap(fn):
    if fn.__closure__ is None:
        return fn
    cells = []
    for c in fn.__closure__:
        try:
            cells.append(types.CellType(c.cell_contents))
        except ValueError:
            cells.append(c)
    g = types.FunctionType(fn.__code__, fn.__globals__, fn.__name__, fn.__defaults__, tuple(cells))
    g.__kwdefaults__ = fn.__kwdefaults__
    return g


def _norm_tokens(toks):
    out = []
    for t in toks:
        if (isinstance(t, tuple) and t[0] == "AT") or t == "KTs":
            out.append("AT")
        elif t == "Vt":
            out += [("WB", 0), ("WB", 1)]
        elif t in ("wsT", "CB"):
            out.append(("WB", 0))
        elif t == "qxT":
            out += [("ET", i) for i in range(4)]
        elif t == "KxT":
            out += [("EU", i) for i in range(3)]
        elif t == "Vx":
            out += [("OS", i) for i in range(4)]
        elif t in ("BR", "MK", "lam4"):
            out.append(("FA", 1))
        else:
            out.append(t)
    return tuple(dict.fromkeys(out))


class Op:
    __slots__ = ("kind", "eng", "fn", "r", "w", "sem", "deps", "signal", "sigval", "inc")

    def __init__(self, kind, eng, fn, r, w, sem=None, inc=16):
        self.kind, self.eng, self.fn, self.r, self.w, self.sem = kind, eng, _snap(fn), _norm_tokens(r), _norm_tokens(w), sem
        self.deps = []
        self.signal = kind != "c"
        self.sigval = None
        self.inc = inc


class Phase:
    def __init__(self, g):
        self.g = g
        self.ops = []

    def op(self, eng, fn, r=(), w=()):
        self.ops.append(Op("c", eng, fn, r, w))

    def dma(self, q, fn, r=(), w=(), sem=None, inc=16):
        self.g.get_dsem(sem)
        self.ops.append(Op("d", q, fn, r, w, sem, inc))

    def emit(self):
        g = self.g
        nc = g.nc
        lastw, readers = {}, {}
        ops = self.ops
        batch_of = {}
        open_batch = {}
        qlast = {}
        for i, o in enumerate(ops):
            deps = set()
            for t in o.r:
                if t in lastw:
                    deps.add(lastw[t])
            for t in o.w:
                if t in lastw:
                    deps.add(lastw[t])
                for rd in readers.get(t, ()):
                    deps.add(rd)
            deps.discard(i)
            if o.kind == "d" and o.inc == 16 and o.sem != "su":
                cur = open_batch.get(o.sem)
                join = False
                if cur and ops[cur[-1]].eng == o.eng and qlast.get(o.eng) == cur[-1]:
                    mem = set(cur)
                    rest = deps - mem
                    raw = any(set(ops[m].w) & set(o.r) for m in cur)
                    if not raw and all(d < cur[0] for d in rest):
                        join = True
                        deps = rest
                if join:
                    cur.append(i)
                else:
                    cur = [i]
                    open_batch[o.sem] = cur
                batch_of[i] = cur
            for d in sorted(deps):
                p = ops[d]
                if p.kind == "c" and o.kind == "c" and p.eng == o.eng == "pe":
                    continue
                if p.kind == "d" and o.kind == "d" and p.sem == o.sem == "su":
                    continue
                o.deps.append(d)
                p.signal = True
            for t in o.r:
                readers.setdefault(t, []).append(i)
            for t in o.w:
                lastw[t] = i
                readers[t] = []
            qlast[o.eng] = i
        for o in ops:
            if o.kind == "c":
                if o.signal:
                    g.ecount[o.eng] += 1
                    o.sigval = g.ecount[o.eng]
            else:
                g.dcount[o.sem] += o.inc
                o.sigval = g.dcount[o.sem]
                assert o.sigval < 65000, "dma sem overflow %s" % o.sem
        for o in ops:
            if o.kind == "d" and o.sem in ("su",):
                o.sigval = g.dcount[o.sem]
        for i, cur in batch_of.items():
            ops[i].sigval = max(ops[i].sigval, ops[cur[-1]].sigval)
        for e in ENGS:
            assert g.ecount[e] < 65000, "eng sem overflow " + e
        per = {e: [o for o in ops if o.eng == e] for e in ENGS}
        handles = {"pe": "tensor", "act": "scalar", "dve": "vector", "pool": "gpsimd", "sp": "sync"}
        with nc.Block() as block:
            for e in ENGS:
                if not per[e]:
                    continue

                def body(eng, e=e):
                    for o in per[e]:
                        need = {}
                        for d in o.deps:
                            p = ops[d]
                            if p.kind == "c":
                                sem, key = g.esem[p.eng], ("e", p.eng)
                            else:
                                sem, key = g.dsem[p.sem], ("d", p.sem)
                            if key not in need or need[key][1] < p.sigval:
                                need[key] = (sem, p.sigval)
                        for key, (sem, val) in need.items():
                            fk = (e, key)
                            if g.frontier.get(fk, 0) < val:
                                eng.wait_ge(sem, val)
                                g.frontier[fk] = val
                        inst = o.fn(eng)
                        if o.kind == "c":
                            if o.signal:
                                inst.then_inc(g.esem[e], 1)
                        else:
                            inst.then_inc(g.dsem[o.sem], o.inc)

                getattr(block, handles[e])(body)
        self.ops = []


def t5_bucket_np(rel):
    n = np.maximum(rel, 0)
    nf = np.maximum(n, 1).astype(np.float32)
    large = 16 + (np.log(nf / np.float32(16)) / np.float32(math.log(128 / 16)) * np.float32(16)).astype(np.int32)
    large = np.minimum(large, 31)
    return np.where(n < 16, n, large)


def build(D, S):
    KC = D // 128
    TOK = S // 8
    NTB = TOK // 128
    TH = min(512, TOK)
    NTH = TOK // TH
    NB = S // 128
    NQT = S // 256
    GW = 12288 + 2 * D
    nc = bass.Bass("TRN2", target_bir_lowering=False)

    def din(name, shape, dt=F32):
        return nc.dram_tensor(name, list(shape), dt, kind="ExternalInput").ap()

    def dsc(name, shape, dt):
        return nc.dram_tensor(name, list(shape), dt)

    x = din("x", [TOK, D])
    mem = din("mem", [256, D])
    wA = din("wA", [D, 2048])
    wtok = din("wtok", [D // 8, GW])
    wpa = din("wpa", [512, D])
    wpb = din("wpb", [512, D])
    wout = din("wout", [D // 8, D])
    wxq = din("wxq", [D // 8, 512])
    wxkv = din("wxkv", [D // 8, 1024])
    wxo = din("wxo", [64, D])
    norm1_g = din("norm1_g", [D])
    norm_x_g = din("norm_x_g", [D])
    norm_mem_g = din("norm_mem_g", [D])
    final_g = din("final_g", [D])
    b_gate = din("b_gate", [2 * D])
    subln_g = din("subln_g", [256])
    sg_ln_g = din("sg_ln_g", [4096])
    sg_ln_b = din("sg_ln_b", [4096])
    sg_w = din("sg_w", [16, 128, 128])
    sg_b = din("sg_b", [16 * 128])
    lamv = din("lamv", [4, 128])
    braw = din("braw", [2, 128, 512])
    b31 = din("b31", [2, 128, 1])
    cmask = din("cmask", [128, 512])
    tril = din("tril", [128, 128])
    ident_d = din("ident", [128, 128], BF16)
    sel_d = din("sel", [128, 8])
    out = nc.dram_tensor("out", [TOK, D], F32, kind="ExternalOutput").ap()

    hnT_own = dsc("hnT_own", [D, TOK], BF16)
    hnT_all = dsc("hnT_all", [8 * D, TOK], BF16)
    wA_b = dsc("wA_b", [D, 2048], BF16)
    QT = dsc("QT", [4, 128, S], BF16)
    KT = dsc("KT", [4, 128, S], BF16)
    Vd = dsc("Vd", [2, S, 256], BF16)
    ZA = dsc("ZA", [S, 512], F32)
    gaT_own = dsc("gaT_own", [512, S], BF16)
    gaT_all = dsc("gaT_all", [4096, S], BF16)
    usT = dsc("usT", [4096, TOK], F32)
    Vg = dsc("Vg", [TOK, 4096], F32)
    gT = dsc("gT", [2 * D, TOK], F32)
    gbT = dsc("gbT", [4096, TOK], BF16)
    mT = dsc("mT", [D, TOK], F32)
    mTb = dsc("mTb", [D, TOK], BF16)
    h1 = dsc("h1", [TOK, D], F32)
    hn2T = dsc("hn2T", [D, TOK], BF16)
    mnT = dsc("mnT", [D, 256], BF16)
    wspec = [
        ("Wu", wtok[:, 0:4096], D // 8, 4096),
        ("Wz", wtok[:, 8192:12288], D // 8, 4096),
        ("Wv", wtok[:, 4096:8192], D // 8, 4096),
        ("Wga", wtok[:, 12288:12288 + D], D // 8, D),
        ("Wgb", wtok[:, 12288 + D:12288 + 2 * D], D // 8, D),
        ("Wpb", wpb, 512, D),
        ("Wpa", wpa, 512, D),
        ("Wout", wout, D // 8, D),
        ("Wxq", wxq, D // 8, 512),
        ("Wxkv", wxkv, D // 8, 1024),
        ("Wxo", wxo, 64, D),
    ]
    Wsl = {n: dsc(n + "_s", [r, c], BF16) for n, _, r, c in wspec}
    Wf = {n: dsc(n + "_f", [8 * r, c], BF16) for n, _, r, c in wspec}

    st = ExitStack()
    with st:
        g = G(nc, st)
        ph = Phase(g)

        def sb(name, shape, dt):
            return st.enter_context(nc.sbuf_tensor("sb_" + name, list(shape), dt))

        AT = sb("AT", [128, 32 * 1024], BF16)
        WB = sb("WB", [128, 2 * 32 * 512], BF16)
        FA = sb("FA", [128, 2, 4096], F32)
        BA = sb("BA", [128, 2, 4096], BF16)
        ident = sb("ident", [128, 128], BF16)
        ones_b = sb("ones_b", [128, 128], BF16)
        gvec = sb("gvec", [128, 3 * KC], F32)
        bgT = sb("bgT", [128, 2 * KC], F32)
        lgT = sb("lgT", [128, 32], F32)
        lbT = sb("lbT", [128, 32], F32)
        SGt = sb("SGt", [128, 256], F32)
        sm = sb("sm", [128, 64], F32)
        Bh = sb("Bh", [128, 2, 512], F32)
        b31s = sb("b31s", [128, 2], F32)
        selt = sb("selt", [128, 8], F32)
        ET = sb("ET", [128, 4, 512], F32)
        EU = sb("EU", [128, 3, 512], F32)
        PT = sb("PT", [128, 3, 512], BF16)
        QTt = sb("QTt", [128, 2, 2, 256], BF16)
        OS = sb("OS", [128, 4, 260], F32)
        GAs = sb("GAs", [128, 1, 2, 256], BF16)
        PS = [st.enter_context(nc.psum_tensor("ps%d" % i, [128, 512], F32)) for i in range(8)]
        PSB = [st.enter_context(nc.psum_tensor("psb%d" % i, [128, 1024], BF16)) for i in range(0)]

        ETb = ET[:].rearrange("p a n -> p (a n)").bitcast(BF16)
        EUb = EU[:].rearrange("p a n -> p (a n)").bitcast(BF16)
        qxT = ETb[:, 0:4 * TOK].rearrange("p (h t) -> p h t", h=4)
        KxT = EUb[:, 0:1024].rearrange("p (h t) -> p h t", h=4)
        OSb = OS[:].rearrange("p a n -> p (a n)").bitcast(BF16)
        Vx = OSb[:, 0:1056].rearrange("p (b h d) -> p b h d", b=2, h=4)
        wsT = WB[:, 12288:14336].rearrange("p (g t) -> p g t", g=16)
        JK = WB[:, 16384:16384 + 4096]
        ETall = [("ET", i) for i in range(4)]
        EUall = [("EU", i) for i in range(3)]
        OSall = [("OS", i) for i in range(4)]
        lam4 = FA[:, 1, 1664:2176].rearrange("p (a n) -> p a n", a=4)
        BR = FA[:, 1, 128:1152].rearrange("p (a n) -> p a n", a=2)
        MK = FA[:, 1, 1152:1664]
        CB = WB[:, 0:8192].bitcast(F32).rearrange("p (c t) -> p c t", c=32)
        SBv = WB[:, 8192:12288].bitcast(F32)

        cnt = {"ps": 0, "u": 0}

        def uid():
            cnt["u"] += 1
            return cnt["u"]

        def ld(dst, src, w, sem="su", q="sp", r=()):
            ph.dma(q, lambda e: e.dma_start(out=dst, in_=src), r=list(r), w=w, sem=sem)

        with nc.allow_non_contiguous_dma(reason="small strided constant loads"):
            ld(ident[:], ident_d, ["ident"])
            ld(gvec[:, 0:KC], norm1_g.rearrange("(k p) -> p k", p=128), ["gvec"])
            ld(gvec[:, KC:2 * KC], norm_x_g.rearrange("(k p) -> p k", p=128), ["gvec"])
            ld(gvec[:, 2 * KC:3 * KC], norm_mem_g.rearrange("(k p) -> p k", p=128), ["gvec"])
            ld(bgT[:], b_gate.rearrange("(k p) -> p k", p=128), ["bgT"])
            ld(lgT[:], sg_ln_g.rearrange("(k p) -> p k", p=128), ["lgT"])
            ld(lbT[:], sg_ln_b.rearrange("(k p) -> p k", p=128), ["lbT"])
            ld(SGt[:], subln_g.partition_broadcast(128), ["SGt"])
            ld(lam4, lamv.partition_broadcast(128), ["lam4", ("FA", 1)])
            ld(BR, braw.rearrange("h p n -> p h n"), ["BR", ("FA", 1)])
            ld(b31s[:], b31.rearrange("h p o -> p (h o)"), ["b31s"])
            ld(MK, cmask, ["MK", ("FA", 1)])
            ld(selt[:], sel_d, ["selt"])
        ph.op("pool", lambda e: e.memset(ones_b[:], 1.0), w=["ones_b"])
        ph.op("pool", lambda e: e.memset(sm[:, 59:60], EPS), w=[("sm", 59)])
        ph.op("dve", lambda e: e.tensor_scalar(out=SGt[:], in0=SGt[:], scalar1=0.8, scalar2=None, op0=ALU.mult), r=["SGt"], w=["SGt"])
        ph.op("dve", lambda e: e.tensor_tensor(out=lam4[:, 0, :], in0=lam4[:, 0, :], in1=lam4[:, 1, :], op=ALU.mult), r=["lam4"], w=["lam4", ("FA", 1)])
        ph.op("dve", lambda e: e.tensor_tensor(out=lam4[:, 2, :], in0=lam4[:, 2, :], in1=lam4[:, 3, :], op=ALU.mult), r=["lam4"], w=["lam4", ("FA", 1)])
        ph.op("dve", lambda e: e.tensor_reduce(out=sm[:, 1:2], in_=lam4[:, 0, :], axis=mybir.AxisListType.X, op=ALU.add), r=["lam4", ("FA", 1)], w=["sm"])
        ph.op("dve", lambda e: e.tensor_reduce(out=sm[:, 2:3], in_=lam4[:, 2, :], axis=mybir.AxisListType.X, op=ALU.add), r=["lam4", ("FA", 1)], w=["sm"])
        ph.op("act", lambda e: e.activation(out=sm[:, 1:3], in_=sm[:, 1:3], func=AF.Exp), r=["sm"], w=["sm"])
        ph.op("dve", lambda e: e.tensor_tensor(out=sm[:, 0:1], in0=sm[:, 2:3], in1=sm[:, 1:2], op=ALU.subtract), r=["sm"], w=["sm"])
        ph.op("dve", lambda e: e.tensor_scalar(out=sm[:, 0:1], in0=sm[:, 0:1], scalar1=-0.2, scalar2=None, op0=ALU.add), r=["sm"], w=["sm"])
        for hl in range(2):
            ph.op("dve", lambda e, hl=hl: e.scalar_tensor_tensor(out=Bh[:, hl, :], in0=BR[:, hl, :], scalar=b31s[:, hl:hl + 1], in1=MK,
                                                                 op0=ALU.subtract, op1=ALU.add), r=["BR", "b31s", "MK", ("FA", 1)], w=["Bh"])
        def rstd_from_ss(ss_ap, n, rtoks):
            ph.op("dve", lambda e: e.tensor_scalar(out=ss_ap, in0=ss_ap, scalar1=1.0 / n, scalar2=EPS, op0=ALU.mult, op1=ALU.add), r=rtoks, w=rtoks)
            ph.op("act", lambda e: e.activation(out=ss_ap, in_=ss_ap, func=AF.Ln), r=rtoks, w=rtoks)
            ph.op("act", lambda e: e.activation(out=ss_ap, in_=ss_ap, func=AF.Exp, scale=-0.5), r=rtoks, w=rtoks)

        def norm_T(src, nrows, goff, dst, rtok=()):
            atv = AT[:, 0:KC * nrows].rearrange("p (k t) -> p k t", k=KC)
            for tb in range(nrows // 128):
                s = tb % 2
                ph.dma("sp", lambda e, tb=tb, s=s: e.dma_start(out=FA[:, s, 0:D], in_=src[tb * 128:(tb + 1) * 128, :]), r=list(rtok), w=[("FA", s)], sem="nl%d" % s)
                ssa = sm[:, 8 + s:9 + s]
                ph.op("act", lambda e, s=s, ssa=ssa: e.activation(out=JK[:, 0:D], in_=FA[:, s, 0:D], func=AF.Square, accum_out=ssa),
                      r=[("FA", s)], w=[("WB", 1), ("sm", 8 + s)])
                rstd_from_ss(ssa, D, [("sm", 8 + s)])
                ph.op("dve", lambda e, s=s, ssa=ssa: e.tensor_scalar(out=BA[:, s, 0:D], in0=FA[:, s, 0:D], scalar1=ssa, scalar2=None, op0=ALU.mult),
                      r=[("FA", s), ("sm", 8 + s)], w=[("BA", s)])
                for k4 in range(KC // 4):
                    bank = 4 + (k4 % 2)
                    ptb = PS[bank][:].bitcast(BF16)
                    for j in range(4):
                        kc = k4 * 4 + j
                        ph.op("pe", lambda e, s=s, kc=kc, j=j, ptb=ptb: e.transpose(ptb[:, j * 128:(j + 1) * 128], BA[:, s, kc * 128:(kc + 1) * 128], ident[:]),
                              r=[("BA", s), "ident"], w=[("PS", bank)])
                    for j in range(4):
                        kc = k4 * 4 + j
                        eng = "dve" if j % 2 == 0 else "act"
                        if eng == "dve":
                            ph.op("dve", lambda e, kc=kc, j=j, ptb=ptb, tb=tb: e.tensor_scalar(
                                out=atv[:, kc, tb * 128:(tb + 1) * 128], in0=ptb[:, j * 128:(j + 1) * 128], scalar1=gvec[:, goff + kc:goff + kc + 1],
                                scalar2=None, op0=ALU.mult), r=[("PS", bank), "gvec"], w=[("AT", kc)])
                        else:
                            ph.op("act", lambda e, kc=kc, j=j, ptb=ptb, tb=tb: e.activation(
                                out=atv[:, kc, tb * 128:(tb + 1) * 128], in_=ptb[:, j * 128:(j + 1) * 128], func=AF.Copy,
                                scale=gvec[:, goff + kc:goff + kc + 1]), r=[("PS", bank), "gvec"], w=[("AT", kc)])
            dv = dst.ap().rearrange("(k p) t -> p k t", p=128)
            for q4 in range(4):
                k0, k1 = q4 * KC // 4, (q4 + 1) * KC // 4
                ph.dma("sp", lambda e, k0=k0, k1=k1: e.dma_start(out=dv[:, k0:k1, :], in_=atv[:, k0:k1, :]),
                       r=[("AT", k) for k in range(k0, k1)], w=[(dst.name, q4)], sem="ns")

        def load_AT(src, r0, kcn, t0, nt, rtoks):
            atv = AT[:, 0:kcn * nt].rearrange("p (k t) -> p k t", k=kcn)
            sv = src.ap()[r0:r0 + kcn * 128, :].rearrange("(k p) t -> p k t", p=128)
            nq = 4 if kcn >= 4 else 1
            for q4 in range(nq):
                k0, k1 = q4 * kcn // nq, (q4 + 1) * kcn // nq
                ph.dma("sp", lambda e, k0=k0, k1=k1: e.dma_start(out=atv[:, k0:k1, :], in_=sv[:, k0:k1, t0:t0 + nt]),
                       r=rtoks, w=[("AT", k) for k in range(k0, k1)], sem="la")
            return atv

        def gemm(atv, kcn, nt, Wd, c0, ncols, mode, epi, wr):
            wv = Wd.ap().rearrange("(k p) n -> p k n", p=128)
            nch = ncols // 512
            attoks = [("AT", k) for k in range(kcn)]

            def wload(ci):
                s = cnt["ws"] = (cnt.get("ws", 0) + 1) % 2
                wbv = WB[:, s * 16384:s * 16384 + kcn * 512].rearrange("p (k n) -> p k n", k=kcn)
                nq = 4 if kcn >= 4 else 1
                for q4 in range(nq):
                    k0, k1 = q4 * kcn // nq, (q4 + 1) * kcn // nq
                    ph.dma("sp", lambda e, k0=k0, k1=k1, ci=ci: e.dma_start(out=wbv[:, k0:k1, :], in_=wv[:, k0:k1, c0 + ci * 512:c0 + (ci + 1) * 512]),
                           r=wr, w=[("WB", s)], sem="w%d" % s)
                return s, wbv

            nxt = wload(0)
            for ci in range(nch):
                s, wbv = nxt
                if ci + 1 < nch:
                    nxt = wload(ci + 1)
                if mode == "T":
                    for tb in range(nt // 128):
                        bank = cnt["ps"] = (cnt["ps"] + 1) % 4
                        ps = PS[bank]
                        for kc in range(kcn):
                            ph.op("pe", lambda e, ps=ps, kc=kc, tb=tb, wbv=wbv: e.matmul(ps[:, :], atv[:, kc, tb * 128:(tb + 1) * 128], wbv[:, kc, :],
                                                                                         start=(kc == 0), stop=(kc == kcn - 1)),
                                  r=[("WB", s)] + attoks, w=[("PS", bank)])
                        epi(ps, ("PS", bank), tb, ci)
                else:
                    th = min(512, nt)
                    for cs in range(4):
                        for hh in range(nt // th):
                            bank = cnt["ps"] = (cnt["ps"] + 1) % 4
                            ps = PS[bank]
                            for kc in range(kcn):
                                ph.op("pe", lambda e, ps=ps, kc=kc, hh=hh, cs=cs, wbv=wbv: e.matmul(
                                    ps[:, 0:th], wbv[:, kc, cs * 128:(cs + 1) * 128], atv[:, kc, hh * th:(hh + 1) * th],
                                    start=(kc == 0), stop=(kc == kcn - 1)), r=[("WB", s)] + attoks, w=[("PS", bank)])
                            epi(ps, ("PS", bank), ci * 4 + cs, hh)

        def et_slot():
            s = cnt["et"] = (cnt.get("et", 0) + 1) % 3
            return s

        def gelu_from(ps_ap, n, ptok, dst_ap, dtok, tmp_ap, ttok):
            ph.op("act", lambda e: e.activation(out=tmp_ap, in_=ps_ap, func=AF.Square), r=[ptok], w=[ttok])
            ph.op("dve", lambda e: e.tensor_scalar(out=tmp_ap, in0=tmp_ap, scalar1=0.044715, scalar2=1.0, op0=ALU.mult, op1=ALU.add), r=[ttok], w=[ttok])
            ph.op("dve", lambda e: e.tensor_tensor(out=tmp_ap, in0=tmp_ap, in1=ps_ap, op=ALU.mult), r=[ttok, ptok], w=[ttok])
            ph.op("act", lambda e: e.activation(out=tmp_ap, in_=tmp_ap, func=AF.Sigmoid, scale=1.5957691216057308), r=[ttok], w=[ttok])
            ph.op("dve", lambda e: e.tensor_tensor(out=dst_ap, in0=tmp_ap, in1=ps_ap, op=ALU.mult), r=[ttok, ptok], w=[dtok])

        def cast_pieces(src_ap, rows, cols, dstt, tag, use_wb=False):
            pr = min(128, rows)
            na = rows // pr
            cw = 2048
            out_ = []
            if rows >= 128:
                sv = src_ap.rearrange("(a p) n -> p a n", p=128)
                dvv = dstt.ap().rearrange("(a p) n -> p a n", p=128)
            for a in range(na):
                for c in range(0, cols, cw):
                    w_ = min(cw, cols - c)
                    if rows >= 128:
                        si, do = sv[:, a, c:c + w_], dvv[:, a, c:c + w_]
                    else:
                        si, do = src_ap[:, c:c + w_], dstt.ap()[:, c:c + w_]
                    out_.append(dict(si=si, do=do, pr=pr, w=w_, wb=use_wb, tag=(tag, len(out_))))
            return out_

        def run_pieces(pl, after):
            def bufs(i):
                p = pl[i]
                s = i % 2
                if p["wb"]:
                    fin = WB[:, s * 4096:(s + 1) * 4096].bitcast(F32)[0:p["pr"], 0:p["w"]]
                    bout = WB[0:p["pr"], 8192 + s * 2048:8192 + s * 2048 + p["w"]]
                    return fin, bout, ("WBc", s), ("WBc", 2 + s), s
                return FA[0:p["pr"], 0, s * 2048:s * 2048 + p["w"]], BA[0:p["pr"], 0, 0:p["w"]], ("FAh", s), ("BA", 0), s

            def load(i):
                fin, bout, tf, tb_, s = bufs(i)
                si = pl[i]["si"]
                ph.dma("pool", lambda e: e.dma_start(out=fin, in_=si), w=[tf], sem="cl%d" % s)

            load(0)
            for i in range(len(pl)):
                if i + 1 < len(pl):
                    load(i + 1)
                fin, bout, tf, tb_, s = bufs(i)
                do = pl[i]["do"]
                ph.op("pool", lambda e: e.tensor_copy(out=bout, in_=fin), r=[tf], w=[tb_])
                ph.dma("pool", lambda e: e.dma_start(out=do, in_=bout), r=[tb_], w=[pl[i]["tag"]], sem="cst%d" % s)
                if i in after:
                    after[i]()

        def allgather(src_t, dst_t, rtok, wtok_, sem):
            ph.dma("pool", lambda e: e.collective_compute("AllGather", ALU.bypass, replica_groups=[list(range(8))],
                                                          ins=[src_t.ap().opt()], outs=[dst_t.ap().opt()]),
                   r=[rtok], w=[wtok_], sem=sem, inc=1)

        pa = cast_pieces(wA, D, 2048, wA_b, "wA_b", use_wb=True)
        run_pieces(pa, {})
        norm_T(x, TOK, 0, hnT_own)
        ph.op("pool", lambda e: e.memset(sm[:, 61:62], 0.0), r=[("WBc", i) for i in range(4)] + [p["tag"] for p in pa],
              w=[("WB", 0), ("WB", 1), ("sm", 61), "wA_b"])
        ph.dma("pool", lambda e: e.collective_compute("AllGather", ALU.bypass, replica_groups=[list(range(8))],
                                                      ins=[hnT_own.ap().opt()], outs=[hnT_all.ap().opt()]),
               r=[("hnT_own", q) for q in range(4)], w=["hnT_all"], sem="cc0", inc=1)
        QTv, KTv = QT.ap(), KT.ap()
        stA_toks = []

        def stA_tok(name):
            stA_toks.append((name, len(stA_toks)))
            return stA_toks[-1]

        for r8 in range(8):
            atv = load_AT(hnT_all, r8 * D, KC, 0, TOK, ["hnT_all"])
            tok0 = r8 * TOK

            def epi_qk(dstv):
                def f(ps, ptok, cc, hh):
                    s = et_slot()
                    ph.op("act" if cc % 2 == 0 else "dve",
                          (lambda e: e.activation(out=PT[:, s % 3, 0:TH], in_=ps[:, 0:TH], func=AF.Copy)) if cc % 2 == 0 else
                          (lambda e: e.tensor_copy(out=PT[:, s % 3, 0:TH], in_=ps[:, 0:TH])), r=[ptok], w=[("PT", s % 3)])
                    ph.dma("sp", lambda e: e.dma_start(out=dstv[cc, :, tok0 + hh * TH:tok0 + (hh + 1) * TH], in_=PT[:, s % 3, 0:TH]),
                           r=[("PT", s % 3)], w=[stA_tok(dstv.tensor.name)], sem="st%d" % (s % 3))
                return f

            def epi_v(ps, ptok, tb, ci):
                s = et_slot()
                ph.op("act", lambda e: e.activation(out=PT[:, s % 3, :], in_=ps[:, :], func=AF.Copy), r=[ptok], w=[("PT", s % 3)])
                ph.dma("sp", lambda e: e.dma_start(out=Vd.ap()[:, tok0 + tb * 128:tok0 + (tb + 1) * 128, :].rearrange("h t d -> t h d"),
                                                     in_=PT[:, s % 3, :].rearrange("p (h d) -> p h d", h=2)),
                       r=[("PT", s % 3)], w=[stA_tok("Vd")], sem="st%d" % (s % 3))

            def epi_z(ps, ptok, tb, ci):
                s = et_slot()
                ph.op("act", lambda e: e.activation(out=EU[:, s, :], in_=ps[:, :], func=AF.Sigmoid), r=[ptok], w=[("EU", s)])
                ph.op("dve", lambda e: e.tensor_tensor(out=ET[:, s, :], in0=EU[:, s, :], in1=ps[:, :], op=ALU.mult), r=[("EU", s), ptok], w=[("ET", s)])
                ph.dma("sp", lambda e: e.dma_start(out=ZA.ap()[tok0 + tb * 128:tok0 + (tb + 1) * 128, :], in_=ET[:, s, :]),
                       r=[("ET", s)], w=[stA_tok("ZA")], sem="se%d" % s)

            gemm(atv, KC, TOK, wA_b, 0, 512, "F", epi_qk(QTv), ["wA_b"])
            gemm(atv, KC, TOK, wA_b, 512, 512, "F", epi_qk(KTv), ["wA_b"])
            gemm(atv, KC, TOK, wA_b, 1024, 512, "T", epi_v, ["wA_b"])
            gemm(atv, KC, TOK, wA_b, 1536, 512, "T", epi_z, ["wA_b"])

        ph.op("dve", lambda e: e.memset(sm[:, 58:59], 0.0), r=list(stA_toks), w=[("sm", 58), "QT", "KT", "Vd", "ZA"])
        ph.op("pool", lambda e: e.memset(sm[:, 62:63], 0.0), r=["QT", "KT", "Vd", "ZA", ("FA", 0)], w=[("sm", 62), ("FAh", 0), ("FAh", 1), ("BA", 0)])
        pw, after = [], {}
        for n, src, r, c in wspec:
            pcs = cast_pieces(src, r, c, Wsl[n], n + "_s")
            for p in pcs:
                p["tag"] = (n + "_s", len(pw))
                pw.append(p)
            tags = [p["tag"] for p in pcs]
            after[len(pw) - 1] = (lambda n=n, tags=tags: ph.dma(
                "pool", lambda e: e.collective_compute("AllGather", ALU.bypass, replica_groups=[list(range(8))],
                                                       ins=[Wsl[n].ap().opt()], outs=[Wf[n].ap().opt()]),
                r=tags, w=[n + "_f"], sem="ccw", inc=1))
        run_pieces(pw, after)
        ph.op("pool", lambda e: e.memset(sm[:, 62:63], 0.0), r=[("FAh", 0), ("FAh", 1)], w=[("sm", 62), ("FA", 0)])

        scale = 128 ** -0.5
        KTs = AT[:, 0:2 * S].rearrange("p (c s) -> p c s", c=2)
        Vt = WB[:, 0:NB * 257].rearrange("p (b d) -> p b d", d=257)
        Ob = [[PS[2], PS[3]], [PS[4], PS[5]]]

        QTs = AT[:, 2 * S:4 * S].rearrange("p (c s) -> p c s", c=2)
        ZAb = FA[:, 1, :].rearrange("p (z n) -> p z n", z=8)
        GAb = ET[:, 2:4, :].rearrange("p a n -> p (a n)").bitcast(BF16).rearrange("p (g a q) -> p g a q", g=4, a=2)
        NZ = 8

        def att_loads(hl, t):
            zs = t % NZ
            ph.dma("sp", lambda e: e.dma_start(out=ZAb[:, zs, :].rearrange("p (j d) -> p j d", j=2),
                                               in_=ZA.ap()[t * 256:(t + 1) * 256, hl * 256:(hl + 1) * 256].rearrange("(j p) d -> p j d", p=128)),
                   r=["ZA"], w=[("ZAb", zs)], sem="bz%d" % zs)

        def att_S(t, kb):
            sbk = kb % 2
            psv = PS[sbk][:].rearrange("p (c q) -> p c q", c=2)
            for c in range(2):
                ph.op("pe", lambda e, c=c: e.matmul(psv[:, c, :], KTs[:, c, kb * 128:(kb + 1) * 128], QTs[:, c, t * 256:(t + 1) * 256], start=True, stop=True),
                      r=[("KQc", (kb * 128) // CS), ("KQc", (t * 256) // CS)], w=[("PS", sbk)])

        def att_exp(hl, t, kb, nkb):
            sbk = kb % 2
            ps = PS[sbk]
            psv = ps[:].rearrange("p (c q) -> p c q", c=2)
            pslot = cnt["pt"] = (cnt.get("pt", 0) + 1) % 3
            if kb >= nkb - 3:
                off = {nkb - 1: 0, nkb - 2: 128, nkb - 3: 256}[kb]
                es = kb % 2
                ph.op("dve", lambda e: e.scalar_tensor_tensor(
                    out=ET[:, es, :].rearrange("p (c q) -> p c q", c=2), in0=psv, scalar=scale,
                    in1=Bh[:, hl, off:off + 256].unsqueeze(1).to_broadcast([128, 2, 256]), op0=ALU.mult, op1=ALU.add),
                    r=[("PS", sbk), "Bh"], w=[("ET", es)])
                ph.op("act", lambda e: e.activation(out=PT[:, pslot, :], in_=ET[:, es, :], func=AF.Exp),
                      r=[("ET", es)], w=[("PT", pslot)])
            else:
                ph.op("act", lambda e: e.activation(out=PT[:, pslot, :], in_=ps[:, :], func=AF.Exp, scale=scale),
                      r=[("PS", sbk)], w=[("PT", pslot)])
            return pslot

        def att_PV(kb, nkb, pslot):
            for c in range(2):
                for j in range(2):
                    ph.op("pe", lambda e, c=c, j=j: e.matmul(
                        Ob[c][j][:, 0:257], PT[:, pslot, c * 256 + j * 128:c * 256 + (j + 1) * 128], Vt[:, kb, :],
                        start=(kb == 0), stop=(kb == nkb - 1)), r=[("PT", pslot), ("Vc", (kb * 128) // CS)], w=[("PS", 2 + c * 2 + j)])

        def att_epi(hl, t):
            zs = t % NZ
            gtk = ("ET", 2 + t % 2)
            gb = (t % 2) * 2 + (t // 2) % 2
            for j in range(2):
                for c in range(2):
                    ph.op("act" if c == 0 else "dve",
                          (lambda e, c=c, j=j: e.activation(out=OS[:, c * 2 + j, 0:257], in_=Ob[c][j][:, 0:257], func=AF.Copy)) if c == 0 else
                          (lambda e, c=c, j=j: e.tensor_copy(out=OS[:, c * 2 + j, 0:257], in_=Ob[c][j][:, 0:257])),
                          r=[("PS", 2 + c * 2 + j)], w=[("OS", c * 2 + j)])
            for j in range(2):
                o1, o2 = OS[:, j, :], OS[:, 2 + j, :]
                rt = [("OS", j), ("OS", 2 + j)]
                sj = sm[:, 16 + 4 * j:20 + 4 * j]
                stok = ("sm", 16 + j)
                ph.op("dve", lambda e, o1=o1, sj=sj: e.reciprocal(out=sj[:, 0:1], in_=o1[:, 256:257]), r=rt, w=[stok])
                ph.op("dve", lambda e, o2=o2, sj=sj: e.reciprocal(out=sj[:, 1:2], in_=o2[:, 256:257]), r=rt, w=[stok])
                ph.op("dve", lambda e, sj=sj: e.tensor_tensor(out=sj[:, 1:2], in0=sj[:, 1:2], in1=sm[:, 0:1], op=ALU.mult), r=[stok, "sm"], w=[stok])
                ph.op("dve", lambda e, o1=o1, sj=sj: e.tensor_scalar(out=o1[:, 0:256], in0=o1[:, 0:256], scalar1=sj[:, 0:1], scalar2=None, op0=ALU.mult), r=rt + [stok], w=rt)
                ph.op("dve", lambda e, o1=o1, o2=o2, sj=sj: e.scalar_tensor_tensor(out=o1[:, 0:256], in0=o2[:, 0:256], scalar=sj[:, 1:2], in1=o1[:, 0:256],
                                                                                 op0=ALU.mult, op1=ALU.add), r=rt + [stok], w=rt)

            def stage1():
                for j in range(2):
                    o1, o2 = OS[:, j, :], OS[:, 2 + j, :]
                    rt = [("OS", j), ("OS", 2 + j)]
                    sj = sm[:, 16 + 4 * j:20 + 4 * j]
                    stok = ("sm", 16 + j)
                    ph.op("act", lambda e, o1=o1, o2=o2, sj=sj: e.activation(out=o2[:, 0:256], in_=o1[:, 0:256], func=AF.Square, accum_out=sj[:, 2:3]), r=rt, w=rt + [stok])
                    ph.op("act", lambda e, sj=sj: e.activation(out=sj[:, 2:3], in_=sj[:, 2:3], func=AF.Ln, scale=1.0 / 256, bias=sm[:, 59:60]), r=[stok, ("sm", 59)], w=[stok])
                    ph.op("act", lambda e, sj=sj: e.activation(out=sj[:, 2:3], in_=sj[:, 2:3], func=AF.Exp, scale=-0.5), r=[stok], w=[stok])

            def stage2():
                for j in range(2):
                    o1 = OS[:, j, :]
                    rt = [("OS", j), ("OS", 2 + j)]
                    sj = sm[:, 16 + 4 * j:20 + 4 * j]
                    stok = ("sm", 16 + j)
                    ph.op("dve", lambda e, o1=o1, sj=sj: e.scalar_tensor_tensor(out=o1[:, 0:256], in0=o1[:, 0:256], scalar=sj[:, 2:3], in1=SGt[:],
                                                                              op0=ALU.mult, op1=ALU.mult), r=rt + [stok, "SGt"], w=rt)
                    ph.op("dve", lambda e, o1=o1, j=j: e.tensor_tensor(out=BA[:, 1, j * 256:(j + 1) * 256], in0=o1[:, 0:256],
                                                                       in1=ZAb[:, zs, j * 256:(j + 1) * 256], op=ALU.mult),
                          r=rt + [("ZAb", zs)], w=[("BA", 1)])

            def stage3():
                ptb = PS[6][:].bitcast(BF16)
                for j in range(2):
                    for dc in range(2):
                        ph.op("pe", lambda e, j=j, dc=dc: e.transpose(ptb[:, dc * 256 + j * 128:dc * 256 + (j + 1) * 128],
                                                                      BA[:, 1, j * 256 + dc * 128:j * 256 + (dc + 1) * 128], ident[:]),
                              r=[("BA", 1), "ident"], w=[("PS", 6)])
                ph.op("act", lambda e: e.activation(out=GAb[:, gb, :, :].rearrange("p a q -> p (a q)"), in_=ptb[:, 0:512], func=AF.Copy),
                      r=[("PS", 6)], w=[gtk])
                ph.dma("sp", lambda e: e.dma_start(
                    out=gaT_own.ap()[hl * 256:(hl + 1) * 256, t * 256:(t + 1) * 256].rearrange("(a p) q -> p a q", p=128), in_=GAb[:, gb, :, :]),
                    r=[gtk], w=[("gaT_own", hl, t)], sem="sg%d" % (t % 2))
            return [stage1, stage2, stage3]

        NCH = 4
        CS = S // NCH
        ctoks = [("KQc", i) for i in range(NCH)] + [("Vc", i) for i in range(NCH)]
        ph.op("dve", lambda e: e.memset(sm[:, 63:64], 0.0), r=["AT", ("WB", 0), ("WB", 1), ("FA", 1)],
              w=[("sm", 63)] + [("ZAb", z) for z in range(NZ)] + ctoks)
        ph.op("dve", lambda e: e.memset(Vt[:, :, 256:257], 1.0), r=[], w=[("Vc", i) for i in range(NCH)])
        nbc = NB // NCH
        for hl in range(2):
            for i in range(NCH):
                for c in range(2):
                    ph.dma("sp", lambda e, c=c, hl=hl, i=i: e.dma_start(out=KTs[:, c, i * CS:(i + 1) * CS], in_=KTv[hl * 2 + c, :, i * CS:(i + 1) * CS]),
                           r=["KT"], w=[("KQc", i)], sem="bc%d" % i)
                for c in range(2):
                    ph.dma("sp", lambda e, c=c, hl=hl, i=i: e.dma_start(out=QTs[:, c, i * CS:(i + 1) * CS], in_=QTv[hl * 2 + c, :, i * CS:(i + 1) * CS]),
                           r=["QT"], w=[("KQc", i)], sem="bc%d" % i)
                ph.dma("sp", lambda e, hl=hl, i=i: e.dma_start(
                    out=Vt[:, i * nbc:(i + 1) * nbc, 0:256], in_=Vd.ap()[hl, i * CS:(i + 1) * CS, :].rearrange("(b p) d -> p b d", p=128)),
                    r=["Vd"], w=[("Vc", i)], sem="bc%d" % i)
            for t0 in range(min(NZ - 2, NQT)):
                att_loads(hl, t0)
            pend = []
            for t in range(NQT):
                if t + NZ - 2 < NQT:
                    att_loads(hl, t + NZ - 2)
                nkb = 2 * t + 2
                att_S(t, 0)
                for kb in range(nkb):
                    if kb + 1 < nkb:
                        att_S(t, kb + 1)
                    pslot = att_exp(hl, t, kb, nkb)
                    att_PV(kb, nkb, pslot)
                    while pend and pend[0][0] <= kb:
                        pend.pop(0)[1]()
                for _, cl in pend:
                    cl()
                parts = att_epi(hl, t)
                pend = [(k_, cl) for k_, cl in zip((6, 10, 14), parts)]
            for _, cl in pend:
                cl()
        ph.op("dve", lambda e: e.memset(sm[:, 63:64], 0.0), r=[("ZAb", z) for z in range(NZ)] + ctoks,
              w=[("sm", 63), ("FA", 1), "AT", ("WB", 0), ("WB", 1)])
        ph.dma("pool", lambda e: e.collective_compute("AllGather", ALU.bypass, replica_groups=[list(range(8))],
                                                      ins=[gaT_own.ap().opt()], outs=[gaT_all.ap().opt()]),
               r=[("gaT_own", hl, t) for hl in range(2) for t in range(NQT)], w=["gaT_all"], sem="cc1", inc=1)

        atv = load_AT(hnT_own, 0, KC, 0, TOK, [("hnT_own", q) for q in range(4)])
        usv, gTv = usT.ap(), gT.ap()

        def epi_u(ps, ptok, cc, hh):
            s = et_slot()
            gelu_from(ps[:, 0:TH], TH, ptok, ET[:, s, 0:TH], ("ET", s), EU[:, s, 0:TH], ("EU", s))
            ph.dma("sp", lambda e: e.dma_start(out=usv[cc * 128:(cc + 1) * 128, hh * TH:(hh + 1) * TH], in_=ET[:, s, 0:TH]),
                   r=[("ET", s)], w=["usT"], sem="se%d" % s)

        def epi_zb(ps, ptok, cc, hh):
            s = et_slot()
            ph.dma("sp", lambda e: e.dma_start(out=ET[:, s, 0:TH], in_=usv[cc * 128:(cc + 1) * 128, hh * TH:(hh + 1) * TH]),
                   r=["usT"], w=[("ET", s)], sem="le%d" % s)
            ph.op("act", lambda e: e.activation(out=EU[:, s, 0:TH], in_=ps[:, 0:TH], func=AF.Sigmoid), r=[ptok], w=[("EU", s)])
            ph.op("dve", lambda e: e.tensor_tensor(out=EU[:, s, 0:TH], in0=EU[:, s, 0:TH], in1=ps[:, 0:TH], op=ALU.mult), r=[("EU", s), ptok], w=[("EU", s)])
            ph.op("dve", lambda e: e.tensor_tensor(out=ET[:, s, 0:TH], in0=ET[:, s, 0:TH], in1=EU[:, s, 0:TH], op=ALU.mult), r=[("EU", s), ("ET", s)], w=[("ET", s)])
            ph.dma("sp", lambda e: e.dma_start(out=usv[cc * 128:(cc + 1) * 128, hh * TH:(hh + 1) * TH], in_=ET[:, s, 0:TH]),
                   r=[("ET", s)], w=["usT2"], sem="se%d" % s)

        def epi_vb(ps, ptok, tb, ci):
            s = et_slot()
            gelu_from(ps[:, :], 512, ptok, ET[:, s, :], ("ET", s), EU[:, s, :], ("EU", s))
            ph.dma("sp", lambda e: e.dma_start(out=Vg.ap()[tb * 128:(tb + 1) * 128, ci * 512:(ci + 1) * 512], in_=ET[:, s, :]),
                   r=[("ET", s)], w=["Vg"], sem="se%d" % s)

        def epi_g(base):
            def f(ps, ptok, cc, hh):
                s = et_slot()
                ph.op("act", lambda e: e.activation(out=ET[:, s, 0:TH], in_=ps[:, 0:TH], func=AF.Sigmoid, bias=bgT[:, base + cc:base + cc + 1]),
                      r=[ptok, "bgT"], w=[("ET", s)])
                ph.dma("sp", lambda e: e.dma_start(out=gTv[(base + cc) * 128:(base + cc + 1) * 128, hh * TH:(hh + 1) * TH], in_=ET[:, s, 0:TH]),
                       r=[("ET", s)], w=["gT"], sem="se%d" % s)
            return f

        gemm(atv, KC, TOK, Wf["Wu"], 0, 4096, "F", epi_u, ["Wu_f"])
        gemm(atv, KC, TOK, Wf["Wz"], 0, 4096, "F", epi_zb, ["Wz_f"])
        gemm(atv, KC, TOK, Wf["Wv"], 0, 4096, "T", epi_vb, ["Wv_f"])
        gemm(atv, KC, TOK, Wf["Wga"], 0, D, "F", epi_g(0), ["Wga_f"])
        gemm(atv, KC, TOK, Wf["Wgb"], 0, D, "F", epi_g(KC), ["Wgb_f"])

        with nc.allow_non_contiguous_dma(reason="sg_w"):
            ld(FA[:, 0, 0:2048].rearrange("p (g s) -> p g s", g=16), sg_w.rearrange("g t s -> t g s"), [("FA", 0)], sem="su2")
            ld(FA[:, 1, 0:128], tril, [("FA", 1)], sem="su4")
        for gi in range(16):
            ph.op("dve", lambda e, gi=gi: e.tensor_tensor(out=BA[:, 0, gi * 128:(gi + 1) * 128], in0=FA[:, 0, gi * 128:(gi + 1) * 128],
                                                          in1=FA[:, 1, 0:128], op=ALU.mult), r=[("FA", 0), ("FA", 1)], w=[("BA", 0)])
        for gi in range(16):
            pt = PS[gi % 2]
            ptb = pt[:].bitcast(BF16)
            ph.op("pe", lambda e, gi=gi, ptb=ptb: e.transpose(ptb[:, 0:128], BA[:, 0, gi * 128:(gi + 1) * 128], ident[:]),
                  r=[("BA", 0), "ident"], w=[("PS", gi % 2)])
            ph.op("act", lambda e, gi=gi, ptb=ptb: e.activation(out=wsT[:, gi, :], in_=ptb[:, 0:128], func=AF.Copy), r=[("PS", gi % 2)], w=["wsT", ("WB", 0)])
        with nc.allow_non_contiguous_dma(reason="broadcast"):
            ph.dma("sp", lambda e: e.dma_start(out=SBv, in_=sg_b.partition_broadcast(128)), w=[("WB", 0)], sem="su5")
        for gi in range(16):
            pt = PS[2 + gi % 2]
            ph.op("pe", lambda e, gi=gi, pt=pt: e.matmul(pt[:, 0:128], ones_b[:], wsT[:, gi, :], start=True, stop=True),
                  r=["ones_b", "wsT", ("WB", 0)], w=[("PS", 2 + gi % 2)])
            for dc in range(2):
                ph.op("dve", lambda e, gi=gi, dc=dc, pt=pt: e.scalar_tensor_tensor(
                    out=CB[:, gi * 2 + dc, :], in0=pt[:, 0:128], scalar=lbT[:, gi * 2 + dc:gi * 2 + dc + 1], in1=SBv[:, gi * 128:(gi + 1) * 128],
                    op0=ALU.mult, op1=ALU.add), r=[("PS", 2 + gi % 2), "lbT", ("WB", 0)], w=["CB", ("WB", 0)])

        vh = AT[:, 0:NTB * 4096].rearrange("p (b n) -> p b n", b=NTB)
        for tb in range(NTB):
            s = tb % 2
            ph.dma("sp", lambda e, tb=tb, s=s: e.dma_start(out=FA[:, s, :], in_=Vg.ap()[tb * 128:(tb + 1) * 128, :]), r=["Vg"], w=[("FA", s)], sem="nl%d" % s)
            sa = sm[:, 24 + 4 * s:28 + 4 * s]
            stok = ("sm", 24 + s)
            ph.op("act", lambda e, s=s, sa=sa: e.activation(out=JK, in_=FA[:, s, :], func=AF.Identity, accum_out=sa[:, 0:1]), r=[("FA", s)], w=[("WB", 1), stok])
            ph.op("act", lambda e, s=s, sa=sa: e.activation(out=JK, in_=FA[:, s, :], func=AF.Square, accum_out=sa[:, 1:2]), r=[("FA", s)], w=[("WB", 1), stok])
            ph.op("dve", lambda e, sa=sa: e.tensor_scalar(out=sa[:, 0:2], in0=sa[:, 0:2], scalar1=1.0 / 4096, scalar2=None, op0=ALU.mult), r=[stok], w=[stok])
            ph.op("dve", lambda e, sa=sa: e.tensor_tensor(out=sa[:, 2:3], in0=sa[:, 0:1], in1=sa[:, 0:1], op=ALU.mult), r=[stok], w=[stok])
            ph.op("dve", lambda e, sa=sa: e.tensor_tensor(out=sa[:, 1:2], in0=sa[:, 1:2], in1=sa[:, 2:3], op=ALU.subtract), r=[stok], w=[stok])
            ph.op("dve", lambda e, sa=sa: e.tensor_scalar(out=sa[:, 1:2], in0=sa[:, 1:2], scalar1=EPS, scalar2=None, op0=ALU.add), r=[stok], w=[stok])
            ph.op("act", lambda e, sa=sa: e.activation(out=sa[:, 1:2], in_=sa[:, 1:2], func=AF.Ln), r=[stok], w=[stok])
            ph.op("act", lambda e, sa=sa: e.activation(out=sa[:, 1:2], in_=sa[:, 1:2], func=AF.Exp, scale=-0.5), r=[stok], w=[stok])
            ph.op("dve", lambda e, s=s, sa=sa, tb=tb: e.tensor_scalar(out=vh[:, tb, :], in0=FA[:, s, :], scalar1=sa[:, 0:1], scalar2=sa[:, 1:2],
                                                                     op0=ALU.subtract, op1=ALU.mult), r=[("FA", s), stok], w=[("AT", k) for k in range(32)])
        TQ = min(4, NTB)
        for gi in range(16):
            for dc in range(2):
                cc = gi * 2 + dc
                for tq in range(NTB // TQ):
                    bank = cnt["ps"] = (cnt["ps"] + 1) % 4
                    ps = PS[bank]
                    for i in range(TQ):
                        tb = tq * TQ + i
                        ph.op("pe", lambda e, ps=ps, i=i, tb=tb, cc=cc, gi=gi: e.matmul(ps[:, i * 128:(i + 1) * 128], vh[:, tb, cc * 128:(cc + 1) * 128], wsT[:, gi, :],
                                                                                         start=True, stop=True),
                              r=[("AT", k) for k in range(32)] + ["wsT", ("WB", 0)], w=[("PS", bank)])
                    s = et_slot()
                    W_ = TQ * 128
                    ph.dma("sp", lambda e, s=s, cc=cc, tq=tq, W_=W_: e.dma_start(out=EU[:, s, 0:W_], in_=usv[cc * 128:(cc + 1) * 128, tq * W_:(tq + 1) * W_]),
                           r=["usT2"], w=[("EU", s)], sem="le%d" % s)
                    ph.op("dve", lambda e, ps=ps, s=s, cc=cc, W_=W_: e.scalar_tensor_tensor(
                        out=ET[:, s, 0:W_].rearrange("p (i t) -> p i t", i=TQ), in0=ps[:, 0:W_].rearrange("p (i t) -> p i t", i=TQ),
                        scalar=lgT[:, cc:cc + 1], in1=CB[:, cc, :].unsqueeze(1).to_broadcast([128, TQ, 128]), op0=ALU.mult, op1=ALU.add),
                        r=[("PS", bank), "lgT", "CB", ("WB", 0)], w=[("ET", s)])
                    ph.op("dve", lambda e, s=s, W_=W_: e.tensor_tensor(out=PT[:, s % 3, 0:W_], in0=ET[:, s, 0:W_], in1=EU[:, s, 0:W_], op=ALU.mult),
                          r=[("ET", s), ("EU", s)], w=[("PT", s % 3)])
                    ph.dma("sp", lambda e, s=s, cc=cc, tq=tq, W_=W_: e.dma_start(out=gbT.ap()[cc * 128:(cc + 1) * 128, tq * W_:(tq + 1) * W_], in_=PT[:, s % 3, 0:W_]),
                           r=[("PT", s % 3)], w=["gbT"], sem="st%d" % (s % 3))

        mTv = mT.ap()

        def epi_pb(ps, ptok, cc, hh):
            s = et_slot()
            ph.dma("sp", lambda e: e.dma_start(out=EU[:, s, 0:TH], in_=gTv[(KC + cc) * 128:(KC + cc + 1) * 128, hh * TH:(hh + 1) * TH]),
                   r=["gT"], w=[("EU", s)], sem="le%d" % s)
            ph.op("dve", lambda e: e.tensor_tensor(out=ET[:, s, 0:TH], in0=EU[:, s, 0:TH], in1=ps[:, 0:TH], op=ALU.mult), r=[("EU", s), ptok], w=[("ET", s)])
            ph.dma("sp", lambda e: e.dma_start(out=mTv[cc * 128:(cc + 1) * 128, hh * TH:(hh + 1) * TH], in_=ET[:, s, 0:TH]),
                   r=[("ET", s)], w=["mT"], sem="se%d" % s)

        def epi_pa(ps, ptok, cc, hh):
            s = et_slot()
            ph.dma("sp", lambda e: e.dma_start(out=EU[:, s, 0:TH], in_=gTv[cc * 128:(cc + 1) * 128, hh * TH:(hh + 1) * TH]),
                   r=["gT"], w=[("EU", s)], sem="le%d" % s)
            ph.dma("sp", lambda e: e.dma_start(out=ET[:, s, 0:TH], in_=mTv[cc * 128:(cc + 1) * 128, hh * TH:(hh + 1) * TH]),
                   r=["mT"], w=[("ET", s)], sem="lf%d" % s)
            ph.op("dve", lambda e: e.tensor_tensor(out=EU[:, s, 0:TH], in0=EU[:, s, 0:TH], in1=ps[:, 0:TH], op=ALU.mult), r=[("EU", s), ptok], w=[("EU", s)])
            ph.op("dve", lambda e: e.tensor_tensor(out=PT[:, s % 3, 0:TH], in0=EU[:, s, 0:TH], in1=ET[:, s, 0:TH], op=ALU.add), r=[("EU", s), ("ET", s)], w=[("PT", s % 3)])
            ph.dma("sp", lambda e: e.dma_start(out=mTb.ap()[cc * 128:(cc + 1) * 128, hh * TH:(hh + 1) * TH], in_=PT[:, s % 3, 0:TH]),
                   r=[("PT", s % 3)], w=["mTb"], sem="st%d" % (s % 3))

        def epi_out(ps, ptok, tb, ci):
            s = et_slot()
            ph.dma("sp", lambda e: e.dma_start(out=EU[:, s, :], in_=x[tb * 128:(tb + 1) * 128, ci * 512:(ci + 1) * 512]), w=[("EU", s)], sem="le%d" % s)
            ph.op("dve", lambda e: e.tensor_tensor(out=ET[:, s, :], in0=EU[:, s, :], in1=ps[:, :], op=ALU.add), r=[("EU", s), ptok], w=[("ET", s)])
            ph.dma("sp", lambda e: e.dma_start(out=h1.ap()[tb * 128:(tb + 1) * 128, ci * 512:(ci + 1) * 512], in_=ET[:, s, :]),
                   r=[("ET", s)], w=["h1"], sem="se%d" % s)

        atv = load_AT(gbT, 0, 32, 0, TOK, ["gbT"])
        gemm(atv, 32, TOK, Wf["Wpb"], 0, D, "F", epi_pb, ["Wpb_f"])

        atv = AT[:, 0:32 * TOK].rearrange("p (k t) -> p k t", k=32)
        gav = gaT_all.ap().rearrange("(k p) s -> p k s", p=128)
        for r8 in range(8):
            for hf in range(2):
                wbv = WB[:, hf * 16384:hf * 16384 + 16 * TOK].rearrange("p (k t) -> p k t", k=16)
                for q2 in range(2):
                    ph.dma("sp", lambda e, r8=r8, hf=hf, q2=q2, wbv=wbv: e.dma_start(
                        out=wbv[:, q2 * 8:(q2 + 1) * 8, :], in_=gav[:, hf * 16 + q2 * 8:hf * 16 + (q2 + 1) * 8, r8 * TOK:(r8 + 1) * TOK]),
                        r=["gaT_all"], w=[("WB", hf)], sem="w%d" % hf)
                ats = [("AT", k) for k in range(hf * 16, hf * 16 + 16)]
                if r8 == 0:
                    ph.op("dve", lambda e, hf=hf, wbv=wbv: e.tensor_scalar(out=atv[:, hf * 16:(hf + 1) * 16, :], in0=wbv, scalar1=selt[:, 0:1], scalar2=None, op0=ALU.mult),
                          r=[("WB", hf), "selt"], w=ats)
                else:
                    ph.op("dve", lambda e, hf=hf, wbv=wbv, r8=r8: e.scalar_tensor_tensor(out=atv[:, hf * 16:(hf + 1) * 16, :], in0=wbv, scalar=selt[:, r8:r8 + 1],
                                                                                       in1=atv[:, hf * 16:(hf + 1) * 16, :], op0=ALU.mult, op1=ALU.add),
                          r=[("WB", hf), "selt"] + ats, w=ats)
        gemm(atv, 32, TOK, Wf["Wpa"], 0, D, "F", epi_pa, ["Wpa_f"])
        atv = load_AT(mTb, 0, KC, 0, TOK, ["mTb"])
        gemm(atv, KC, TOK, Wf["Wout"], 0, D, "T", epi_out, ["Wout_f"])

        norm_T(mem, 256, 2 * KC, mnT)
        atm = AT[:, 0:KC * 256].rearrange("p (k t) -> p k t", k=KC)

        def epi_kx(ps, ptok, cc, hh):
            ph.op("act", lambda e: e.activation(out=KxT[:, cc, :], in_=ps[:, 0:256], func=AF.Copy), r=[ptok], w=["KxT"] + EUall)

        def epi_vx(ps, ptok, tb, ci):
            ph.op("act", lambda e: e.activation(out=Vx[:, tb, :, 0:128], in_=ps[:, :].rearrange("p (h d) -> p h d", h=4), func=AF.Copy), r=[ptok], w=["Vx"] + OSall)

        ph.op("pool", lambda e: e.memset(Vx[:, :, :, 128:132], 1.0), w=["Vx"] + OSall)
        gemm(atm, KC, 256, Wf["Wxkv"], 0, 512, "F", epi_kx, ["Wxkv_f"])
        gemm(atm, KC, 256, Wf["Wxkv"], 512, 512, "T", epi_vx, ["Wxkv_f"])
        norm_T(h1, TOK, KC, hn2T, rtok=["h1"])
        atv = AT[:, 0:KC * TOK].rearrange("p (k t) -> p k t", k=KC)

        def epi_qx(ps, ptok, cc, hh):
            ph.op("act", lambda e: e.activation(out=qxT[:, cc, hh * TH:(hh + 1) * TH], in_=ps[:, 0:TH], func=AF.Copy), r=[ptok], w=["qxT"] + ETall)

        gemm(atv, KC, TOK, Wf["Wxq"], 0, 512, "F", epi_qx, ["Wxq_f"])
        oxT = AT[:, 0:4 * TOK].rearrange("p (k t) -> p k t", k=4)
        for hh in range(NTH):
            for h in range(4):
                pts = []
                for kb in range(2):
                    ps = PS[kb]
                    ph.op("pe", lambda e, ps=ps, kb=kb, h=h, hh=hh: e.matmul(ps[:, 0:TH], KxT[:, h, kb * 128:(kb + 1) * 128], qxT[:, h, hh * TH:(hh + 1) * TH], start=True, stop=True),
                          r=["KxT", "qxT"], w=[("PS", kb)])
                    pslot = cnt["pt"] = (cnt.get("pt", 0) + 1) % 3
                    ph.op("act", lambda e, ps=ps, pslot=pslot: e.activation(out=PT[:, pslot, 0:TH], in_=ps[:, 0:TH], func=AF.Exp, scale=scale), r=[("PS", kb)], w=[("PT", pslot)])
                    pts.append(pslot)
                for j in range(TH // 128):
                    po = PS[2 + j]
                    for kb in range(2):
                        ph.op("pe", lambda e, po=po, j=j, kb=kb, h=h, pts=pts: e.matmul(po[:, 0:129], PT[:, pts[kb], j * 128:(j + 1) * 128], Vx[:, kb, h, 0:129],
                                                                                      start=(kb == 0), stop=(kb == 1)), r=[("PT", pts[kb]), "Vx"], w=[("PS", 2 + j)])
                    sj = sm[:, 40 + j:41 + j]
                    ph.op("dve", lambda e, po=po, sj=sj: e.reciprocal(out=sj, in_=po[:, 128:129]), r=[("PS", 2 + j)], w=[("sm", 40 + j)])
                    ph.op("dve", lambda e, po=po, sj=sj, j=j, h=h: e.tensor_scalar(out=BA[:, 0, j * 512 + h * 128:j * 512 + (h + 1) * 128], in0=po[:, 0:128], scalar1=sj, scalar2=None, op0=ALU.mult),
                          r=[("PS", 2 + j), ("sm", 40 + j)], w=[("BA", 0)])
            for j in range(TH // 128):
                ptb = PS[6 + j % 2][:].bitcast(BF16)
                for h in range(4):
                    ph.op("pe", lambda e, j=j, h=h, ptb=ptb: e.transpose(ptb[:, h * 128:(h + 1) * 128], BA[:, 0, j * 512 + h * 128:j * 512 + (h + 1) * 128], ident[:]),
                          r=[("BA", 0), "ident"], w=[("PS", 6 + j % 2)])
                tk0 = hh * TH + j * 128
                ph.op("act", lambda e, ptb=ptb, tk0=tk0: e.activation(out=oxT[:, :, tk0:tk0 + 128], in_=ptb[:, 0:512].rearrange("p (h t) -> p h t", h=4), func=AF.Copy),
                      r=[("PS", 6 + j % 2)], w=[("AT", k) for k in range(KC)])
        wxo_sb = WB[:, 0:4 * D].rearrange("p (k n) -> p k n", k=4)
        ph.dma("sp", lambda e: e.dma_start(out=wxo_sb, in_=Wf["Wxo"].ap().rearrange("(k p) n -> p k n", p=128)), r=["Wxo_f"], w=[("WB", 0)], sem="w0")
        FGv = WB[:, 16384:16384 + 2 * D].bitcast(F32)
        with nc.allow_non_contiguous_dma(reason="broadcast"):
            ph.dma("sp", lambda e: e.dma_start(out=FGv, in_=final_g.partition_broadcast(128)), w=[("WB", 1)], sem="su3")
        for tb in range(NTB):
            s = tb % 2
            ph.dma("sp", lambda e, tb=tb, s=s: e.dma_start(out=FA[:, s, 0:D], in_=h1.ap()[tb * 128:(tb + 1) * 128, :]), r=["h1"], w=[("FA", s)], sem="nl%d" % s)
            for ci in range(D // 512):
                bank = cnt["ps"] = (cnt["ps"] + 1) % 4
                ps = PS[bank]
                for kc in range(4):
                    ph.op("pe", lambda e, ps=ps, kc=kc, tb=tb, ci=ci: e.matmul(ps[:, :], oxT[:, kc, tb * 128:(tb + 1) * 128], wxo_sb[:, kc, ci * 512:(ci + 1) * 512],
                                                                                start=(kc == 0), stop=(kc == 3)),
                          r=[("AT", k) for k in range(KC)] + [("WB", 0)], w=[("PS", bank)])
                ph.op("dve", lambda e, ps=ps, s=s, ci=ci: e.tensor_tensor(out=FA[:, s, ci * 512:(ci + 1) * 512], in0=FA[:, s, ci * 512:(ci + 1) * 512], in1=ps[:, :], op=ALU.add),
                      r=[("PS", bank), ("FA", s)], w=[("FA", s)])
            ssa = sm[:, 48 + s:49 + s]
            fo = AT[:, 16384 + s * 8192:16384 + s * 8192 + 2 * D].bitcast(F32)
            ftok = [("AT", 16 + s * 8 + k) for k in range(8)]
            ph.op("act", lambda e, s=s, ssa=ssa, fo=fo: e.activation(out=fo, in_=FA[:, s, 0:D], func=AF.Square, accum_out=ssa), r=[("FA", s)], w=ftok + [("sm", 48 + s)])
            rstd_from_ss(ssa, D, [("sm", 48 + s)])
            ph.op("dve", lambda e, s=s, ssa=ssa, fo=fo: e.scalar_tensor_tensor(out=fo, in0=FA[:, s, 0:D], scalar=ssa, in1=FGv, op0=ALU.mult, op1=ALU.mult),
                  r=[("FA", s), ("sm", 48 + s), ("WB", 1)], w=ftok)
            ph.dma("sp", lambda e, tb=tb, fo=fo: e.dma_start(out=out[tb * 128:(tb + 1) * 128, :], in_=fo), r=ftok, w=["out"], sem="so")
        ph.op("pool", lambda e: e.memset(sm[:, 60:61], 0.0), r=["out"], w=[("sm", 60)])
        with nc.allow_non_contiguous_dma(reason="small strided constant loads / layout stores"):
            ph.emit()
        build.dsem_names = list(g.dsem.keys())
    return nc


def make_in_maps(D, S, x, mem, w_in, b_gate, norm1_g, lam_q1, lam_k1, lam_q2, lam_k2, subln_g, rel_bias, sg_ln_g, sg_ln_b,
                 sg_w, sg_b, w_proj_a, w_proj_b, w_out, norm_x_g, norm_mem_g, w_xq, w_xkv, w_xo, final_g):
    f = np.float32
    TOK = S // 8
    x2 = np.asarray(x, f).reshape(S, D)
    mem2 = np.ascontiguousarray(np.asarray(mem, f).reshape(256, D))
    w_in = np.asarray(w_in, f)[0]
    rel_bias = np.asarray(rel_bias, f)
    kk = np.arange(128)[:, None]
    qq = np.arange(128)[None, :]
    b0 = t5_bucket_np(qq - kk)
    b1 = t5_bucket_np(128 + qq - kk)
    cm = np.zeros((128, 512), f)
    cm[:, 0:128] = NEG
    cm[:, 128:256] = np.where(qq - kk >= 0, 0.0, NEG)
    ident = np.eye(128, dtype=f).astype(ml_dtypes.bfloat16)
    tril = np.tril(np.ones((128, 128), f))
    lamv = np.stack([np.asarray(v, f)[0] for v in (lam_q1, lam_k1, lam_q2, lam_k2)], 0)
    wpa = np.asarray(w_proj_a, f)[0]
    wpb = np.asarray(w_proj_b, f)[0]
    wo = np.asarray(w_out, f)[0]
    wq = np.asarray(w_xq, f)[0]
    wkv = np.asarray(w_xkv, f)[0]
    wxo = np.asarray(w_xo, f)[0]
    maps = []
    for c in range(8):
        hs = [2 * c, 2 * c + 1]
        cols = []
        for base in (0, 4096, 8192, 12288):
            for h in hs:
                cols.append(np.arange(base + h * 256, base + (h + 1) * 256))
        cols = np.concatenate(cols)
        braw = np.empty((2, 128, 512), f)
        b31 = np.empty((2, 128, 1), f)
        for i, h in enumerate(hs):
            braw[i, :, 0:128] = rel_bias[31, h]
            braw[i, :, 128:256] = rel_bias[b0, h]
            braw[i, :, 256:384] = rel_bias[b1, h]
            braw[i, :, 384:512] = rel_bias[31, h]
            b31[i, :, 0] = rel_bias[31, h]
        sel = np.zeros((128, 8), f)
        sel[:, c] = 1.0
        r0, r1 = c * D // 8, (c + 1) * D // 8
        maps.append({
            "x": np.ascontiguousarray(x2[c * TOK:(c + 1) * TOK]),
            "mem": mem2,
            "wA": np.ascontiguousarray(w_in[:, cols]),
            "wtok": np.ascontiguousarray(w_in[r0:r1, 16384:]),
            "wpa": np.ascontiguousarray(wpa[c * 512:(c + 1) * 512]),
            "wpb": np.ascontiguousarray(wpb[c * 512:(c + 1) * 512]),
            "wout": np.ascontiguousarray(wo[r0:r1]),
            "wxq": np.ascontiguousarray(wq[r0:r1]),
            "wxkv": np.ascontiguousarray(wkv[r0:r1]),
            "wxo": np.ascontiguousarray(wxo[c * 64:(c + 1) * 64]),
            "norm1_g": np.asarray(norm1_g, f)[0], "norm_x_g": np.asarray(norm_x_g, f)[0], "norm_mem_g": np.asarray(norm_mem_g, f)[0],
            "final_g": np.asarray(final_g, f), "b_gate": np.asarray(b_gate, f)[0], "subln_g": np.asarray(subln_g, f)[0],
            "sg_ln_g": np.asarray(sg_ln_g, f)[0], "sg_ln_b": np.asarray(sg_ln_b, f)[0],
            "sg_w": np.ascontiguousarray(np.asarray(sg_w, f)[0]), "sg_b": np.ascontiguousarray(np.asarray(sg_b, f)[0].reshape(-1)),
            "lamv": lamv, "braw": braw, "b31": b31, "cmask": cm, "tril": tril, "ident": ident, "sel": sel,
        })
    return maps


_NC_CACHE = {}


def run_kernel(D, S, inputs, trace=False):
    key = (D, S)
    if key not in _NC_CACHE:
        _NC_CACHE[key] = build(D, S)
    nc = _NC_CACHE[key]
    inputs = dict(inputs)
    inputs.pop("positions", None)
    maps = make_in_maps(D, S, **inputs)
    res = run_bass_kernel_spmd(nc, maps, core_ids=list(range(8)), trace=trace)
    outp = np.concatenate([r["out"] for r in res.results], axis=0).reshape(1, S, D).astype(np.float32)
    return outp, res


def kernel(**inputs):
    outp, _ = run_kernel(4096, 8192, inputs)
    return outp
```
